# Optimizing a Trainium2 kernel written in Bass

```python
import math
import jax
import jax.numpy as jnp
from jax import lax
import numpy as np

D_MODEL = 1024
BATCH = 8
SEQ = 4096
DEPTH = 4

GRID_W = 64
CTX_LEN = 256
ROPE_BASE = 10000.0
NORM_EPS = 1e-6
BLOCK = 128
NEG_INF = -1e30

MLA_HEADS = 6
MLA_Q_RANK = 256
MLA_KV_RANK = 128
MLA_NOPE = 64
MLA_ROPE = 32
MLA_V = 64
SSM_HEADS = 6
SSM_HEAD_DIM = 64
SSM_D_INNER = SSM_HEADS * SSM_HEAD_DIM
SSM_GROUPS = 2
SSM_STATE = 128
SSM_CONV = 3
SSM_CHUNK = 128
SSM_CONV_CH = SSM_D_INNER + 2 * SSM_GROUPS * SSM_STATE
SWA_HEADS = 4
SWA_KV_HEADS = 2
SWA_HEAD_DIM = 64
WINDOW = 128
FFN_HIDDEN = 2816
FFN_CONV = 3

MIX_WIDTH = MLA_HEADS * MLA_V + SSM_D_INNER + SWA_HEADS * SWA_HEAD_DIM
IN_SPLITS = (MLA_Q_RANK, MLA_KV_RANK, MLA_ROPE,
             SSM_D_INNER, SSM_CONV_CH, 2 * SSM_HEADS,
             SWA_HEADS * SWA_HEAD_DIM, SWA_KV_HEADS * SWA_HEAD_DIM, SWA_KV_HEADS * SWA_HEAD_DIM)
IN_WIDTH = sum(IN_SPLITS)

kernel_name = 'hybrid_mla_ssd_swa_diffusion_block'


def rms_norm(x, g):
    xf = x.astype(jnp.float32)
    y = xf * lax.rsqrt(jnp.mean(xf * xf, axis=-1, keepdims=True) + NORM_EPS)
    return (y * g.astype(jnp.float32)).astype(x.dtype)


def modulate(h, shift, scale):
    return h * (1 + scale) + shift


def split_cols(p, sizes):
    idx = np.cumsum(np.array(sizes))[:-1].tolist()
    return jnp.split(p, idx, axis=-1)


def dwconv(x, w, b):
    k, ch = w.shape
    y = lax.conv_general_dilated(x, w[:, None, :].astype(x.dtype), window_strides=(1,),
                                 padding=[((k - 1) // 2, k // 2)],
                                 dimension_numbers=('NWC', 'WIO', 'NWC'), feature_group_count=ch)
    return y + b.astype(x.dtype)


def rope_1d(x, pos):
    n = x.shape[-1] // 2
    inv = jnp.power(ROPE_BASE, -jnp.arange(n, dtype=jnp.float32) / n)
    ang = pos.astype(jnp.float32)[:, None] * inv
    cos, sin = jnp.cos(ang)[:, None, :], jnp.sin(ang)[:, None, :]
    xf = x.astype(jnp.float32)
    x1, x2 = xf[..., :n], xf[..., n:]
    return jnp.concatenate([x1 * cos - x2 * sin, x1 * sin + x2 * cos], axis=-1).astype(x.dtype)


def axial_rope(x, row, col):
    h = x.shape[-1] // 2
    return jnp.concatenate([rope_1d(x[..., :h], row), rope_1d(x[..., h:], col)], axis=-1)


def joint_softmax(logits):
    sizes = [l.shape[-1] for l in logits]
    p = jax.nn.softmax(jnp.concatenate([l.astype(jnp.float32) for l in logits], axis=-1), axis=-1)
    return split_cols(p, sizes)


def mla_queries(qa, g, w_uq, pos):
    b, l, _ = qa.shape
    q = (rms_norm(qa, g) @ w_uq).reshape(b, l, MLA_HEADS, MLA_NOPE + MLA_ROPE)
    if pos is not None:
        q = jnp.concatenate([q[..., :MLA_NOPE], axial_rope(q[..., MLA_NOPE:], *pos)], axis=-1)
    return q * (MLA_NOPE + MLA_ROPE) ** -0.5


def mla_keys_values(kva, kr, g, w_ukv, pos):
    b, l, _ = kva.shape
    kv = (rms_norm(kva, g) @ w_ukv).reshape(b, l, MLA_HEADS, MLA_NOPE + MLA_V)
    k_pe = kr[:, :, None, :]
    if pos is not None:
        k_pe = axial_rope(k_pe, *pos)
    k = jnp.concatenate([kv[..., :MLA_NOPE], jnp.broadcast_to(k_pe, (b, l, MLA_HEADS, MLA_ROPE))], axis=-1)
    return k, kv[..., MLA_NOPE:]


def mla_latent_attention(q, k, v, kc, vc):
    b, s, h, dq = q.shape
    nb = s // BLOCK
    qb = jnp.moveaxis(q.reshape(b, nb, BLOCK, h, dq), 1, 0)

    def one_block(qblk):
        lc = jnp.einsum('bqhd,bkhd->bhqk', qblk, kc)
        ll = jnp.einsum('bqhd,bkhd->bhqk', qblk, k)
        pc, pl = joint_softmax([lc, ll])
        return (jnp.einsum('bhqk,bkhd->bqhd', pc.astype(v.dtype), vc)
                + jnp.einsum('bhqk,bkhd->bqhd', pl.astype(v.dtype), v))

    out = lax.map(one_block, qb)
    return jnp.moveaxis(out, 0, 1).reshape(b, s, h * MLA_V)


def mla_context_attention(qc, kc, vc):
    b, n, h, _ = qc.shape
    p = jax.nn.softmax(jnp.einsum('bqhd,bkhd->bhqk', qc, kc).astype(jnp.float32), axis=-1)
    return jnp.einsum('bhqk,bkhd->bqhd', p.astype(vc.dtype), vc).reshape(b, n, h * MLA_V)


def ssd_inputs(xbc, dt_raw, conv_w, conv_b, dt_bias):
    b, l, _ = xbc.shape
    xbc = jax.nn.silu(dwconv(xbc, conv_w, conv_b))
    xs, bm, cm = split_cols(xbc, (SSM_D_INNER, SSM_GROUPS * SSM_STATE, SSM_GROUPS * SSM_STATE))
    dt = jax.nn.softplus(dt_raw.astype(jnp.float32).reshape(b, l, 2, SSM_HEADS)
                         + dt_bias.astype(jnp.float32))
    return (xs.reshape(b, l, SSM_HEADS, SSM_HEAD_DIM),
            bm.reshape(b, l, SSM_GROUPS, SSM_STATE),
            cm.reshape(b, l, SSM_GROUPS, SSM_STATE), dt)


def ssd_chunked_scan(x, dt, a, bm, cm, h0, need_y):
    b, l, h, p = x.shape
    g, n = bm.shape[-2:]
    r, q = h // g, SSM_CHUNK
    nc = l // q
    f32 = jnp.float32
    xdt = (x.astype(f32) * dt[..., None]).reshape(b, nc, q, g, r, p)
    acs = jnp.cumsum((dt * a).reshape(b, nc, q, g, r), axis=2)
    bc = bm.astype(f32).reshape(b, nc, q, g, n)
    cc = cm.astype(f32).reshape(b, nc, q, g, n)
    a_last = acs[:, :, -1]
    states = jnp.einsum('bcqgn,bcqgr,bcqgrp->bcgrpn', bc, jnp.exp(a_last[:, :, None] - acs), xdt)

    def carry(hs, inp):
        st, al = inp
        return hs * jnp.exp(al)[..., None, None] + st, hs

    h_final, h_start = lax.scan(carry, h0, (jnp.moveaxis(states, 1, 0), jnp.moveaxis(a_last, 1, 0)))
    if not need_y:
        return None, h_final
    h_start = jnp.moveaxis(h_start, 0, 1)
    y_off = jnp.einsum('bcqgn,bcgrpn,bcqgr->bcqgrp', cc, h_start, jnp.exp(acs))
    acs_t = jnp.moveaxis(acs, 2, -1)
    diff = acs_t[..., :, None] - acs_t[..., None, :]
    tril = jnp.tril(jnp.ones((q, q), dtype=bool))
    decay = jnp.where(tril, jnp.exp(jnp.where(tril, diff, 0.0)), 0.0)
    cb = jnp.einsum('bcqgn,bckgn->bcgqk', cc, bc)
    y_diag = jnp.einsum('bcgqk,bcgrqk,bckgrp->bcqgrp', cb, decay, xdt)
    return (y_diag + y_off).reshape(b, l, h, p).astype(x.dtype), h_final


def ssd_bidirectional(xs, bm, cm, dt, a, h0, need_y):
    ys, finals = [], []
    for d in range(2):
        f = (lambda t: jnp.flip(t, axis=1)) if d == 1 else (lambda t: t)
        y, h_t = ssd_chunked_scan(f(xs), f(dt[:, :, d]), a[d], f(bm), f(cm), h0[d], need_y)
        ys.append(f(y) if need_y else None)
        finals.append(h_t)
    y = ys[0] + ys[1] if need_y else None
    return y, (finals[0], finals[1])


def ssd_output(y, xs, z, d_skip, norm_g):
    b, l = z.shape[:2]
    y = (y + d_skip[:, None].astype(y.dtype) * xs).reshape(b, l, SSM_D_INNER)
    return rms_norm(y * jax.nn.silu(z), norm_g)


def swa_latent_attention(q, k, v, kc, vc, sink):
    b, s, hq, d = q.shape
    hk = k.shape[2]
    r, nb = hq // hk, s // BLOCK
    qb = q.reshape(b, nb, BLOCK, hk, r, d)

    def band(t):
        tp = jnp.pad(t.reshape(b, nb, BLOCK, hk, d), ((0, 0), (1, 1), (0, 0), (0, 0), (0, 0)))
        return jnp.concatenate([tp[:, :-2], tp[:, 1:-1], tp[:, 2:]], axis=2)

    kb, vb = band(k), band(v)
    blk = jnp.arange(nb)[:, None, None]
    qpos = blk * BLOCK + jnp.arange(BLOCK)[None, :, None]
    kpos = (blk - 1) * BLOCK + jnp.arange(3 * BLOCK)[None, None, :]
    valid = (jnp.abs(kpos - qpos) <= WINDOW) & (kpos >= 0) & (kpos < s)
    lb = jnp.einsum('bnqgrd,bnkgd->bngrqk', qb, kb).astype(jnp.float32)
    lb = jnp.where(valid[None, :, None, None], lb, NEG_INF)
    lc = jnp.einsum('bnqgrd,bkgd->bngrqk', qb, kc)
    ls = jnp.broadcast_to(sink.reshape(hk, r)[None, None, :, :, None, None].astype(jnp.float32),
                          lc.shape[:-1] + (1,))
    _, pc, pb = joint_softmax([ls, lc, lb])
    out = (jnp.einsum('bngrqk,bkgd->bnqgrd', pc.astype(v.dtype), vc)
           + jnp.einsum('bngrqk,bnkgd->bnqgrd', pb.astype(v.dtype), vb))
    return out.reshape(b, s, hq * d)


def swa_context_attention(qc, kc, vc, sink):
    b, n, hq, d = qc.shape
    hk = kc.shape[2]
    r = hq // hk
    lc = jnp.einsum('bqgrd,bkgd->bgrqk', qc.reshape(b, n, hk, r, d), kc)
    ls = jnp.broadcast_to(sink.reshape(hk, r)[None, :, :, None, None].astype(jnp.float32),
                          lc.shape[:-1] + (1,))
    _, pc = joint_softmax([ls, lc])
    return jnp.einsum('bgrqk,bkgd->bqgrd', pc.astype(vc.dtype), vc).reshape(b, n, hq * d)


def conv_glu(h, w_up, conv_w, conv_b, w_down):
    val, gate = jnp.split(h @ w_up, 2, axis=-1)
    return (jax.nn.silu(dwconv(gate, conv_w, conv_b)) * val) @ w_down


def hybrid_layer(x, xc, mod, modc, p, pos, update_ctx):
    b, s, _ = x.shape
    sh1, sc1, g1, sh2, sc2, g2 = [m[:, None, :] for m in jnp.split(mod, 6, axis=-1)]
    shc1, scc1, gc1, shc2, scc2, gc2 = jnp.split(modc, 6, axis=-1)

    h = modulate(rms_norm(x, p['norm1_g']), sh1, sc1)
    hc = modulate(rms_norm(xc, p['norm1_g']), shc1, scc1)
    qa, kva, kr, z, xbc, dtr, swq, swk, swv = split_cols(h @ p['w_in'], IN_SPLITS)
    qac, kvac, krc, zc, xbcc, dtrc, swqc, swkc, swvc = split_cols(hc @ p['w_in'], IN_SPLITS)

    ka_c, va_c = mla_keys_values(kvac, krc, p['mla_kv_norm_g'], p['mla_w_ukv'], None)
    ka, va = mla_keys_values(kva, kr, p['mla_kv_norm_g'], p['mla_w_ukv'], pos)
    qa_l = mla_queries(qa, p['mla_q_norm_g'], p['mla_w_uq'], pos)
    y_a = mla_latent_attention(qa_l, ka, va, ka_c, va_c)

    a = -jnp.exp(p['ssm_a_log'].astype(jnp.float32))
    xs_c, bm_c, cm_c, dt_c = ssd_inputs(xbcc, dtrc, p['ssm_conv_w'], p['ssm_conv_b'], p['ssm_dt_bias'])
    xs, bm, cm, dt = ssd_inputs(xbc, dtr, p['ssm_conv_w'], p['ssm_conv_b'], p['ssm_dt_bias'])
    h0 = jnp.zeros((b, SSM_GROUPS, SSM_HEADS // SSM_GROUPS, SSM_HEAD_DIM, SSM_STATE), jnp.float32)
    yb_c, h_ctx = ssd_bidirectional(xs_c, bm_c, cm_c, dt_c, a, (h0, h0), update_ctx)
    yb, _ = ssd_bidirectional(xs, bm, cm, dt, a, h_ctx, True)
    y_b = ssd_output(yb, xs, z, p['ssm_d'], p['ssm_norm_g'])

    scale = SWA_HEAD_DIM ** -0.5
    n = xc.shape[1]
    qs = axial_rope(swq.reshape(b, s, SWA_HEADS, SWA_HEAD_DIM), *pos) * scale
    ks = axial_rope(swk.reshape(b, s, SWA_KV_HEADS, SWA_HEAD_DIM), *pos)
    vs = swv.reshape(b, s, SWA_KV_HEADS, SWA_HEAD_DIM)
    ks_c = swkc.reshape(b, n, SWA_KV_HEADS, SWA_HEAD_DIM)
    vs_c = swvc.reshape(b, n, SWA_KV_HEADS, SWA_HEAD_DIM)
    y_c = swa_latent_attention(qs, ks, vs, ks_c, vs_c, p['swa_sink'])

    x = x + g1 * (jnp.concatenate([y_a, y_b, y_c], axis=-1) @ p['w_out'])
    h2 = modulate(rms_norm(x, p['norm2_g']), sh2, sc2)
    x = x + g2 * conv_glu(h2, p['ffn_w_up'], p['ffn_conv_w'], p['ffn_conv_b'], p['ffn_w_down'])

    if update_ctx:
        ya_c = mla_context_attention(mla_queries(qac, p['mla_q_norm_g'], p['mla_w_uq'], None), ka_c, va_c)
        yb_c = ssd_output(yb_c, xs_c, zc, p['ssm_d'], p['ssm_norm_g'])
        yc_c = swa_context_attention(swqc.reshape(b, n, SWA_HEADS, SWA_HEAD_DIM) * scale, ks_c, vs_c, p['swa_sink'])
        xc = xc + gc1 * (jnp.concatenate([ya_c, yb_c, yc_c], axis=-1) @ p['w_out'])
        hc2 = modulate(rms_norm(xc, p['norm2_g']), shc2, scc2)
        xc = xc + gc2 * conv_glu(hc2, p['ffn_w_up'], p['ffn_conv_w'], p['ffn_conv_b'], p['ffn_w_down'])
    return x, xc


def setup_inputs(seed: int = 0) -> dict:
    key = jax.random.key(seed)
    keys = iter(jax.random.split(key, 40))
    d, nl, f = D_MODEL, DEPTH, FFN_HIDDEN

    def normal(shape, scale):
        return jax.random.normal(next(keys), shape, jnp.float32) * scale

    def gain(shape):
        return 1.0 + normal(shape, 0.02)

    dt0 = jnp.exp(jax.random.uniform(next(keys), (nl, 2, SSM_HEADS), jnp.float32,
                                     math.log(1e-3), math.log(1e-1)))
    return {
        'x': normal((BATCH, SEQ, d), 1.0),
        'c': normal((BATCH, d), 1.0),
        'ctx': normal((BATCH, CTX_LEN, d), 1.0),
        'c_ctx': normal((d,), 1.0),
        'w_mod': normal((nl, d, 6 * d), 0.5 * d ** -0.5),
        'b_mod': normal((nl, 6 * d), 0.02),
        'norm1_g': gain((nl, d)),
        'norm2_g': gain((nl, d)),
        'w_in': normal((nl, d, IN_WIDTH), d ** -0.5),
        'mla_q_norm_g': gain((nl, MLA_Q_RANK)),
        'mla_kv_norm_g': gain((nl, MLA_KV_RANK)),
        'mla_w_uq': normal((nl, MLA_Q_RANK, MLA_HEADS * (MLA_NOPE + MLA_ROPE)), MLA_Q_RANK ** -0.5),
        'mla_w_ukv': normal((nl, MLA_KV_RANK, MLA_HEADS * (MLA_NOPE + MLA_V)), MLA_KV_RANK ** -0.5),
        'ssm_conv_w': normal((nl, SSM_CONV, SSM_CONV_CH), SSM_CONV ** -0.5),
        'ssm_conv_b': normal((nl, SSM_CONV_CH), 0.02),
        'ssm_dt_bias': dt0 + jnp.log(-jnp.expm1(-dt0)),
        'ssm_a_log': jnp.log(jax.random.uniform(next(keys), (nl, 2, SSM_HEADS), jnp.float32, 1.0, 16.0)),
        'ssm_d': gain((nl, SSM_HEADS)),
        'ssm_norm_g': gain((nl, SSM_D_INNER)),
        'swa_sink': normal((nl, SWA_HEADS), 0.5),
        'w_out': normal((nl, MIX_WIDTH, d), MIX_WIDTH ** -0.5),
        'ffn_w_up': normal((nl, d, 2 * f), d ** -0.5),
        'ffn_conv_w': normal((nl, FFN_CONV, f), FFN_CONV ** -0.5),
        'ffn_conv_b': normal((nl, f), 0.02),
        'ffn_w_down': normal((nl, f, d), f ** -0.5),
        'final_norm_g': gain((d,)),
    }


def reference(x, c, ctx, c_ctx, w_mod, b_mod, norm1_g, norm2_g, w_in, mla_q_norm_g, mla_kv_norm_g,
              mla_w_uq, mla_w_ukv, ssm_conv_w, ssm_conv_b, ssm_dt_bias, ssm_a_log, ssm_d, ssm_norm_g,
              swa_sink, w_out, ffn_w_up, ffn_conv_w, ffn_conv_b, ffn_w_down, final_norm_g):
    s = x.shape[1]
    rows = s // GRID_W
    pos = (jnp.repeat(jnp.arange(rows), GRID_W), jnp.tile(jnp.arange(GRID_W), rows))
    silu_c = jax.nn.silu(c)
    silu_cc = jax.nn.silu(c_ctx)
    xc = ctx
    for l in range(DEPTH):
        mod = silu_c @ w_mod[l] + b_mod[l]
        modc = silu_cc @ w_mod[l] + b_mod[l]
        p = dict(norm1_g=norm1_g[l], norm2_g=norm2_g[l], w_in=w_in[l],
                 mla_q_norm_g=mla_q_norm_g[l], mla_kv_norm_g=mla_kv_norm_g[l],
                 mla_w_uq=mla_w_uq[l], mla_w_ukv=mla_w_ukv[l],
                 ssm_conv_w=ssm_conv_w[l], ssm_conv_b=ssm_conv_b[l], ssm_dt_bias=ssm_dt_bias[l],
                 ssm_a_log=ssm_a_log[l], ssm_d=ssm_d[l], ssm_norm_g=ssm_norm_g[l],
                 swa_sink=swa_sink[l], w_out=w_out[l], ffn_w_up=ffn_w_up[l],
                 ffn_conv_w=ffn_conv_w[l], ffn_conv_b=ffn_conv_b[l], ffn_w_down=ffn_w_down[l])
        x, xc = hybrid_layer(x, xc, mod, modc, p, pos, update_ctx=(l < DEPTH - 1))
    return rms_norm(x, final_norm_g)
```

```python
import contextlib
import math
import numpy as np
import concourse.bass as bass
import concourse.mybir as mybir
from concourse.bass_utils import run_bass_kernel_spmd

F32 = mybir.dt.float32
BF16 = mybir.dt.bfloat16
AF = mybir.ActivationFunctionType
ALU = mybir.AluOpType

D = 1024
KC = 8
CTX = 256
EPS = 1e-6
NA = 2892
FFN = 2816
FC = 22


class Res:
    __slots__ = ("name", "w_eng", "w_dma", "r_eng", "r_dma", "excl", "reg", "hist")
    ALL = []

    def __init__(self, name="", excl=False, reg=False):
        self.name = name
        self.excl = excl
        self.reg = reg
        self.clear()
        Res.ALL.append(self)

    def clear(self):
        self.w_eng = {}
        self.w_dma = set()
        self.r_eng = {}
        self.r_dma = set()
        self.hist = []


def _bbox(ap):
    dims = ap.ap
    off = ap.offset
    pstep, pcnt = dims[0]
    if pstep <= 0:
        return (0, 128, 0, 1 << 30)
    p0 = off // pstep
    f0 = off % pstep
    span = 1
    for st, c in dims[1:]:
        span += (c - 1) * abs(st)
    return (p0, p0 + pcnt, f0, f0 + span)


class V:
    __slots__ = ("ap", "res")

    def __init__(self, ap, res):
        self.ap = ap
        self.res = res if isinstance(res, (list, tuple)) else [res]

    def __getitem__(self, idx):
        return V(self.ap[idx], self.res)

    def re(self, pattern, **kw):
        return V(self.ap.rearrange(pattern, **kw), self.res)

    def bc(self, shape):
        return V(self.ap.to_broadcast(list(shape)), self.res)


NDMA = 48
CENG = ("pe", "dve", "act", "pool")


class Sched:
    def __init__(self, nc):
        self.nc = nc
        self.es = contextlib.ExitStack()
        self.scopes = []
        self.ops = []
        self.seq = {e: 0 for e in CENG}
        self.sig = {e: 0 for e in CENG}
        self.ndma = 0
        self.eng = {"pe": nc.tensor, "dve": nc.vector, "act": nc.scalar, "pool": nc.gpsimd, "sp": nc.sync}
        self.sems = {e: self.es.enter_context(nc.semaphore("sem_" + e)) for e in CENG}
        self.dsems = [self.es.enter_context(nc.semaphore("dsem%d" % i)) for i in range(NDMA)]
        self.bar = self.es.enter_context(nc.semaphore("bar"))
        self.nbar = 0
        self.seen_eng = {e: {} for e in self.eng}
        self.seen_dma = {e: set() for e in self.eng}
        self.dma_done_upto = 0
        self.rkeys = {}
        self.nwait = 0
        self.nops = 0
        self.marks = []
        self.dummy = None

    def _stack(self):
        return self.scopes[-1] if self.scopes else self.es

    def sb(self, name, shape, dt):
        t = self._stack().enter_context(self.nc.sbuf_tensor(name + "_%d" % len(Res.ALL), list(shape), dt))
        return V(t[:], Res(name, reg=True))

    def ps(self, name, shape, dt=F32):
        t = self._stack().enter_context(self.nc.psum_tensor(name, list(shape), dt))
        return V(t[:], Res(name, excl=True))

    def R(self, *key):
        r = self.rkeys.get(key)
        if r is None:
            r = Res(str(key))
            self.rkeys[key] = r
        return r

    def push(self):
        self.scopes.append(contextlib.ExitStack())

    def pop(self, tag=""):
        self.flush()
        self.scopes.pop().close()
        self.marks.append((tag, dict(self.seq), self.ndma))

    def _rec(self, eng, fn, reads, writes, dma=False):
        deps_eng = {}
        deps_dma = set()

        def add_eng(d):
            for e, s in d.items():
                if s > deps_eng.get(e, 0):
                    deps_eng[e] = s

        def add_who(who):
            if who[0] == "dma":
                deps_dma.add(who[1])
            elif who[2] > deps_eng.get(who[1], 0):
                deps_eng[who[1]] = who[2]

        rres = []
        wres = []
        rreg = []
        wreg = []
        for v in reads:
            if isinstance(v, V):
                for r in v.res:
                    if r.reg:
                        rreg.append((r, _bbox(v.ap)))
                    else:
                        (wres if r.excl else rres).append(r)
        for v in writes:
            for r in v.res:
                if r.reg:
                    wreg.append((r, _bbox(v.ap)))
                else:
                    wres.append(r)
        for r in rres:
            add_eng(r.w_eng)
            deps_dma |= r.w_dma
        for r in wres:
            add_eng(r.w_eng)
            deps_dma |= r.w_dma
            add_eng(r.r_eng)
            deps_dma |= r.r_dma
        for r, bx in rreg:
            for (p0, p1, f0, f1, kind, who) in r.hist:
                if kind == "w" and p0 < bx[1] and bx[0] < p1 and f0 < bx[3] and bx[2] < f1:
                    add_who(who)
        for r, bx in wreg:
            for (p0, p1, f0, f1, kind, who) in r.hist:
                if p0 < bx[1] and bx[0] < p1 and f0 < bx[3] and bx[2] < f1:
                    add_who(who)
        if dma:
            did = self.ndma
            self.ndma += 1
            if did >= NDMA:
                deps_dma.add(did - NDMA)
            me = ("dma", did)
            who_me = ("dma", did)
        else:
            self.seq[eng] += 1
            me = (eng, self.seq[eng])
            who_me = ("eng", eng, self.seq[eng])
            if eng == "pe":
                deps_eng.pop("pe", None)
        for r in wres:
            r.clear()
            if dma:
                r.w_dma.add(me[1])
            else:
                r.w_eng[eng] = me[1]
        for r in rres:
            if r in wres:
                continue
            if dma:
                r.r_dma.add(me[1])
            else:
                if me[1] > r.r_eng.get(eng, 0):
                    r.r_eng[eng] = me[1]
        for r, bx in wreg:
            r.hist = [h for h in r.hist if not (bx[0] <= h[0] and h[1] <= bx[1] and bx[2] <= h[2] and h[3] <= bx[3])]
            r.hist.append((bx[0], bx[1], bx[2], bx[3], "w", who_me))
        for r, bx in rreg:
            if not dma:
                r.hist = [h for h in r.hist if not (h[4] == "r" and h[5][0] == "eng" and h[5][1] == eng
                                                    and h[0] == bx[0] and h[1] == bx[1] and h[2] == bx[2] and h[3] == bx[3])]
            r.hist.append((bx[0], bx[1], bx[2], bx[3], "r", who_me))
        self.ops.append((eng, fn, deps_eng, deps_dma, me))

    def flush(self, final=False):
        nc = self.nc
        need = {e: set() for e in CENG}
        last = {}
        for eng, fn, deps_eng, deps_dma, me in self.ops:
            for e, s in deps_eng.items():
                need[e].add(s)
            if me[0] != "dma":
                last[me[0]] = me[1]
        for e, s in last.items():
            need[e].add(s)
        cnt = {}
        for e in CENG:
            c = self.sig[e]
            m = {}
            for s in sorted(need[e]):
                c += 1
                m[s] = c
            cnt[e] = m
            self.sig[e] = c
        for eng, fn, deps_eng, deps_dma, me in self.ops:
            E = self.eng[eng]
            for e, s in deps_eng.items():
                c = cnt[e][s]
                if self.seen_eng[eng].get(e, 0) >= c:
                    continue
                E.wait_ge(self.sems[e], c)
                self.nwait += 1
                self.seen_eng[eng][e] = c
            for d in sorted(deps_dma):
                if d < self.dma_done_upto or d in self.seen_dma[eng]:
                    continue
                E.wait_ge(self.dsems[d % NDMA], 16 * (d // NDMA + 1))
                self.nwait += 1
                self.seen_dma[eng].add(d)
            ins = fn()
            self.nops += 1
            if me[0] == "dma":
                ins.then_inc(self.dsems[me[1] % NDMA], 16)
            elif me[1] in cnt[me[0]]:
                ins.then_inc(self.sems[me[0]], 1)
        self.ops = []
        self.nbar += 1
        sp = self.eng["sp"]
        for dd in range(max(self.dma_done_upto, self.ndma - NDMA), self.ndma):
            if dd in self.seen_dma["sp"]:
                continue
            sp.wait_ge(self.dsems[dd % NDMA], 16 * (dd // NDMA + 1))
        sp.sem_inc(self.bar, 1)
        for en, E in self.eng.items():
            for e2 in CENG:
                if self.seen_eng[en].get(e2, 0) < self.sig[e2]:
                    E.wait_ge(self.sems[e2], self.sig[e2])
                    self.seen_eng[en][e2] = self.sig[e2]
            E.wait_ge(self.bar, self.nbar)
        self.dma_done_upto = self.ndma
        for e in self.eng:
            self.seen_dma[e] = set()
        for r in Res.ALL:
            r.clear()

    def finish(self):
        self.flush(final=True)
        while self.scopes:
            self.scopes.pop().close()
        self.es.close()

    def mm(self, out, lhsT, rhs, start=True, stop=True):
        nc = self.nc
        self._rec("pe", lambda: nc.tensor.matmul(out.ap, lhsT.ap, rhs.ap, start=start, stop=stop),
                  [lhsT, rhs], [out])

    def tr(self, out, in_, ident):
        nc = self.nc
        self._rec("pe", lambda: nc.tensor.transpose(out.ap, in_.ap, ident.ap), [in_, ident], [out])

    def act(self, out, in_, func, bias=None, scale=1.0):
        nc = self.nc
        kw = {}
        if bias is not None:
            kw["bias"] = bias.ap if isinstance(bias, V) else bias
        sc = scale.ap if isinstance(scale, V) else scale
        self._rec("act", lambda: nc.scalar.activation(out.ap, in_.ap, func, scale=sc, **kw),
                  [in_, bias, scale], [out])

    def tt(self, eng, out, in0, in1, op):
        E = self.eng[eng]
        self._rec(eng, lambda: E.tensor_tensor(out.ap, in0.ap, in1.ap, op), [in0, in1], [out])

    def ts(self, eng, out, in0, s1, s2=None, op0=ALU.mult, op1=None):
        E = self.eng[eng]
        a1 = s1.ap if isinstance(s1, V) else s1
        a2 = s2.ap if isinstance(s2, V) else s2
        kw = {}
        if op1 is not None:
            kw["op1"] = op1
        self._rec(eng, lambda: E.tensor_scalar(out.ap, in0.ap, a1, a2, op0, **kw), [in0, s1, s2], [out])

    def stt(self, eng, out, in0, scalar, in1, op0, op1):
        E = self.eng[eng]
        a = scalar.ap if isinstance(scalar, V) else scalar
        self._rec(eng, lambda: E.scalar_tensor_tensor(out.ap, in0.ap, a, in1.ap, op0, op1),
                  [in0, scalar, in1], [out])

    def copy(self, eng, out, in_):
        if eng == "act":
            nc = self.nc
            self._rec("act", lambda: nc.scalar.copy(out.ap, in_.ap), [in_], [out])
        else:
            E = self.eng[eng]
            self._rec(eng, lambda: E.tensor_copy(out.ap, in_.ap), [in_], [out])

    def memset(self, eng, out, val):
        E = self.eng[eng]
        self._rec(eng, lambda: E.memset(out.ap, val), [], [out])

    def recip(self, out, in_):
        nc = self.nc
        self._rec("dve", lambda: nc.vector.reciprocal(out.ap, in_.ap), [in_], [out])

    def dma(self, q, out, in_):
        E = self.eng[q]
        self._rec(q, lambda: E.dma_start(out=out.ap, in_=in_.ap), [in_], [out], dma=True)


def _perm(n2):
    n = n2 // 2
    return np.array([j + n if j < n else j - n for j in range(n2)])


def _axial_perm(d):
    h = d // 2
    p = _perm(h)
    return np.concatenate([p, p + h])


def _rope_tables(d, nlat, grid_w):
    h = d // 2
    n = h // 2
    t = np.arange(nlat)
    row = (t // grid_w).astype(np.float32)
    col = (t % grid_w).astype(np.float32)
    inv = np.power(np.float32(10000.0), -np.arange(n, dtype=np.float32) / np.float32(n)).astype(np.float32)
    C = np.zeros((d, nlat), np.float32)
    Sg = np.zeros((d, nlat), np.float32)
    for half, pos in ((0, row), (1, col)):
        ang = (pos[None, :] * inv[:, None]).astype(np.float32)
        c, s = np.cos(ang), np.sin(ang)
        b = half * h
        C[b:b + n] = c
        C[b + n:b + h] = c
        Sg[b:b + n] = -s
        Sg[b + n:b + h] = s
    return C, Sg


def fm(v, nchunk):
    sh = v.shape[:-1]
    return np.ascontiguousarray(np.swapaxes(v.reshape(sh + (nchunk, 128)), -1, -2))


def prep_inputs(inp, nlat, depth, ncores, grid_w=64):
    T = CTX + nlat
    L = depth
    f32 = np.float32
    w_in = np.asarray(inp["w_in"], f32)[:L]
    o_qa, o_kva, o_kr, o_z, o_xbc, o_dt, o_swq, o_swk, o_swv = np.cumsum([0, 256, 128, 32, 384, 896, 12, 256, 128])
    p32, p64 = _axial_perm(32), _axial_perm(64)
    cols = []
    cols += list(range(o_qa, o_qa + 256))
    cols += list(range(o_kva, o_kva + 128))
    cols += list(range(o_kr, o_kr + 32))
    cols += list(o_kr + p32)
    cols += list(range(o_z, o_z + 384))
    cols += list(range(o_xbc, o_xbc + 896))
    cols += list(range(o_swq, o_swq + 256))
    for h in range(4):
        cols += list(o_swq + h * 64 + p64)
    for g in range(2):
        kk = list(range(o_swk + g * 64, o_swk + g * 64 + 64))
        cols += kk + kk
    for g in range(2):
        kk = list(o_swk + g * 64 + p64)
        cols += kk + kk
    cols += list(range(o_swv, o_swv + 128))
    cols += list(range(o_dt, o_dt + 12))
    cols = np.array(cols)
    assert len(cols) == NA
    w_inA = np.ascontiguousarray(w_in[:, :, cols])
    w_uq = np.asarray(inp["mla_w_uq"], f32)[:L]
    pq = np.arange(576)
    for h in range(6):
        pq[h * 96 + 64:h * 96 + 96] = h * 96 + 64 + p32
    w_uq2 = np.ascontiguousarray(np.stack([w_uq, w_uq[:, :, pq]], axis=2))
    w_ukv = np.asarray(inp["mla_w_ukv"], f32)[:L].reshape(L, 128, 6, 128)
    w_ukvk = np.ascontiguousarray(w_ukv[:, :, :, :64].reshape(L, 128, 384))
    w_ukvv = np.ascontiguousarray(w_ukv[:, :, :, 64:].reshape(L, 128, 384))
    cq = np.ones((96, T), f32)
    sq = np.zeros((96, T), f32)
    c32, s32 = _rope_tables(32, nlat, grid_w)
    cq[64:, CTX:] = c32
    sq[64:, CTX:] = s32
    c64, s64 = _rope_tables(64, nlat, grid_w)
    cs = np.ones((128, T), f32)
    ss = np.zeros((128, T), f32)
    cs[:64, CTX:] = c64
    cs[64:, CTX:] = c64
    ss[:64, CTX:] = s64
    ss[64:, CTX:] = s64
    jj = np.arange(128)
    umat = (jj[:, None] <= jj[None, :]).astype(f32)
    lmat = (jj[:, None] >= jj[None, :]).astype(f32)

    def rep(v):
        v = np.asarray(v, f32)[:L].reshape(L, 1, -1)
        return np.ascontiguousarray(np.broadcast_to(v, (L, 128, v.shape[-1])))

    ssm_d = np.asarray(inp["ssm_d"], f32)[:L]
    dcol = np.zeros((L, 128, 3), f32)
    for c in range(3):
        dcol[:, :64, c] = ssm_d[:, 2 * c:2 * c + 1]
        dcol[:, 64:, c] = ssm_d[:, 2 * c + 1:2 * c + 2]
    shared = {
        "w_mod": np.ascontiguousarray(np.asarray(inp["w_mod"], f32)[:L]),
        "b_modT": fm(np.asarray(inp["b_mod"], f32)[:L], 48),
        "n1g": fm(np.asarray(inp["norm1_g"], f32)[:L], 8),
        "n2g": fm(np.asarray(inp["norm2_g"], f32)[:L], 8),
        "fng": fm(np.asarray(inp["final_norm_g"], f32), 8),
        "w_inA": w_inA,
        "gq": fm(np.asarray(inp["mla_q_norm_g"], f32)[:L], 2),
        "gkv": fm(np.asarray(inp["mla_kv_norm_g"], f32)[:L], 1),
        "w_uq2": w_uq2,
        "w_ukvk": w_ukvk,
        "w_ukvv": w_ukvv,
        "cw": np.ascontiguousarray(np.transpose(
            np.asarray(inp["ssm_conv_w"], f32)[:L].reshape(L, 3, 7, 128), (0, 3, 2, 1))),
        "cb": fm(np.asarray(inp["ssm_conv_b"], f32)[:L], 7),
        "dtb": rep(np.asarray(inp["ssm_dt_bias"], f32)[:L].reshape(L, 12)),
        "alog": rep(np.asarray(inp["ssm_a_log"], f32)[:L].reshape(L, 12)),
        "dcol": dcol,
        "sng": fm(np.asarray(inp["ssm_norm_g"], f32)[:L], 3),
        "sink": rep(inp["swa_sink"]),
        "w_out": np.ascontiguousarray(np.asarray(inp["w_out"], f32)[:L]),
        "w_up": np.ascontiguousarray(np.asarray(inp["ffn_w_up"], f32)[:L]),
        "fcw": np.ascontiguousarray(np.transpose(
            np.asarray(inp["ffn_conv_w"], f32)[:L].reshape(L, 3, FC, 128), (0, 3, 2, 1))),
        "fcb": fm(np.asarray(inp["ffn_conv_b"], f32)[:L], FC),
        "w_dn": np.ascontiguousarray(np.asarray(inp["ffn_w_down"], f32)[:L]),
        "cq": cq, "sq": sq, "cs": cs, "ss": ss,
        "ident": np.eye(128, dtype=f32), "umat": umat, "lmat": lmat,
    }
    x = np.asarray(inp["x"], f32)
    c = np.asarray(inp["c"], f32)
    ctx = np.asarray(inp["ctx"], f32)
    c_ctx = np.asarray(inp["c_ctx"], f32)
    maps = []
    for b in range(ncores):
        cv = np.stack([fm(c[b], 8), fm(c_ctx, 8)], axis=-1)
        m = dict(shared)
        m["x_in"] = np.ascontiguousarray(x[b, :nlat])
        m["ctx_in"] = np.ascontiguousarray(ctx[b])
        m["cvec"] = np.ascontiguousarray(cv)
        maps.append(m)
    return maps


def build_program(nlat, depth, debug_outs=(), stop_after=None):
    L = depth
    T = CTX + nlat
    NB = T // 128
    groups = [(0, CTX, 1)] + [(CTX + 512 * j, 512, 0) for j in range(nlat // 512)]
    NG = len(groups)
    Res.ALL = []
    nc = bass.Bass("TRN2", target_bir_lowering=False)
    S = Sched(nc)
    RO = Res("readonly")

    def din(name, shape):
        return V(nc.dram_tensor(name, list(shape), F32, kind="ExternalInput").ap(), RO)

    x_in = din("x_in", [nlat, D])
    ctx_in = din("ctx_in", [CTX, D])
    cvec = din("cvec", [128, 8, 2])
    w_mod = din("w_mod", [L, D, 6 * D])
    b_modT = din("b_modT", [L, 128, 48])
    n1g_d = din("n1g", [L, 128, 8])
    n2g_d = din("n2g", [L, 128, 8])
    fng_d = din("fng", [128, 8])
    w_inA = din("w_inA", [L, D, NA])
    gq_d = din("gq", [L, 128, 2])
    gkv_d = din("gkv", [L, 128, 1])
    w_uq2 = din("w_uq2", [L, 256, 2, 576])
    w_ukvk = din("w_ukvk", [L, 128, 384])
    w_ukvv = din("w_ukvv", [L, 128, 384])
    cw_d = din("cw", [L, 128, 7, 3])
    cb_d = din("cb", [L, 128, 7])
    dtb_d = din("dtb", [L, 128, 12])
    alog_d = din("alog", [L, 128, 12])
    dcol_d = din("dcol", [L, 128, 3])
    sng_d = din("sng", [L, 128, 3])
    sink_d = din("sink", [L, 128, 4])
    w_out = din("w_out", [L, D, D])
    w_up = din("w_up", [L, D, 2 * FFN])
    fcw_d = din("fcw", [L, 128, FC, 3])
    fcb_d = din("fcb", [L, 128, FC])
    w_dn = din("w_dn", [L, FFN, D])
    cq_d = din("cq", [96, T])
    sq_d = din("sq", [96, T])
    cs_d = din("cs", [128, T])
    ss_d = din("ss", [128, T])
    ident_d = din("ident", [128, 128])
    umat_d = din("umat", [128, 128])
    lmat_d = din("lmat", [128, 128])
    out_d = nc.dram_tensor("out", [nlat, D], F32, kind="ExternalOutput").ap()

    dbg = {}

    def scratch(name, shape, dt):
        kind = "ExternalOutput" if name in debug_outs else "Internal"
        return nc.dram_tensor(name, list(shape), dt, kind=kind).ap()

    XT = scratch("XT", [8, 128, T], F32)
    QT = scratch("QT", [6, 96, T], BF16)
    KT = scratch("KT", [6, 96, T], BF16)
    VA = scratch("VA", [NB, 128, 390], BF16)
    QS = scratch("QS", [2, 128, T], BF16)
    KS = scratch("KS", [2, 128, T], BF16)
    VS = scratch("VS", [NB, 128, 130], BF16)
    ZT = scratch("ZT", [3, 128, T], BF16)
    XBC = scratch("XBC", [7, 128, T], F32)
    DTD = scratch("DTD", [NB, 128, 12], F32)
    YT = scratch("YT", [8, 128, T], BF16)
    YF = scratch("YF", [3, 128, T], F32)
    H2 = scratch("H2", [8, 128, T], BF16)
    GV = scratch("GV", [2 * FC, 128, T], BF16)

    def dv(ap, *key):
        return V(ap, S.R(*key))

    ident = S.sb("ident", [128, 128], F32)
    umat = S.sb("umat", [128, 128], F32)
    lmat = S.sb("lmat", [128, 128], F32)
    identb = S.sb("identb", [128, 128], BF16)
    umatb = S.sb("umatb", [128, 128], BF16)
    lmatb = S.sb("lmatb", [128, 128], BF16)
    onesf = S.sb("onesf", [128, 128], F32)
    onesb = S.sb("onesb", [128, 128], BF16)
    scv = S.sb("scv", [128, 8, 2], F32)
    epsb = S.sb("epsb", [128, 1], F32)
    modT = S.sb("modT", [128, 48, 2], F32)
    gsc = S.sb("gsc", [128, 2, 8, 2], F32)
    PS = [S.ps("psb%d" % i, [128, 512], F32) for i in range(7)]
    PSB = S.ps("psbf", [128, 1024], BF16)
    S.dma("sp", ident, ident_d)
    S.dma("sp", umat, umat_d)
    S.dma("sp", lmat, lmat_d)
    S.dma("sp", scv, cvec)
    S.copy("dve", identb, ident)
    S.copy("dve", umatb, umat)
    S.copy("dve", lmatb, lmat)
    S.memset("pool", onesf, 1.0)
    S.memset("pool", onesb, 1.0)
    S.memset("pool", epsb, EPS)
    S.act(scv, scv, AF.Silu)
    S.flush()

    wstg = []
    wcnt = [0]

    def alloc_wstg():
        wstg[:] = [S.sb("wstg%d" % i, [128, 2048], F32) for i in range(3)]

    def load_w1(dst, src, w):
        st = wstg[wcnt[0] % 3]
        eng = "act" if wcnt[0] % 2 == 0 else "dve"
        wcnt[0] += 1
        S.dma("sp", st[:, :w], src)
        S.copy(eng, dst, st[:, :w])

    def load_w(dst, src, ncols):
        for c0 in range(0, ncols, 2048):
            w = min(2048, ncols - c0)
            load_w1(dst[:, c0:c0 + w], src[:, c0:c0 + w], w)

    def load_w_kc(dst, src_l, ncols):
        for c0 in range(0, ncols, 2048):
            w = min(2048, ncols - c0)
            for kc in range(8):
                load_w1(dst[:, kc, c0:c0 + w], src_l[kc * 128:(kc + 1) * 128, c0:c0 + w], w)

    psi = [0]
    nrot = [6]

    def nps():
        psi[0] = (psi[0] + 1) % nrot[0]
        return PS[psi[0]]

    def rms_rstd(chunks, n, dim, sqt, rstd):
        ps = nps()
        for i, ch in enumerate(chunks):
            S.act(sqt[:, i, :n], ch, AF.Square)
            S.mm(ps[:, :n], onesb, sqt[:, i, :n], start=(i == 0), stop=(i == len(chunks) - 1))
        S.act(rstd[:, :n], ps[:, :n], AF.Ln, bias=epsb[:, 0:1], scale=1.0 / dim)
        S.act(rstd[:, :n], rstd[:, :n], AF.Exp, scale=-0.5)

    def xt_view(g):
        t0, n, j = groups[g]
        return dv(XT[:, :, t0:t0 + n].rearrange("k p t -> p k t"), "XT", g)

    S.push()
    xin_t = [S.sb("xin%d" % i, [128, D], F32) for i in range(2)]
    xst = [S.sb("xst%d" % i, [128, 8, 512], F32) for i in range(2)]
    bi = 0
    for g, (t0, n, j) in enumerate(groups):
        st = xst[g % 2]
        for tb in range(n // 128):
            xi = xin_t[bi % 2]
            bi += 1
            src = ctx_in[tb * 128:(tb + 1) * 128, :] if j == 1 else \
                x_in[t0 - CTX + tb * 128:t0 - CTX + (tb + 1) * 128, :]
            S.dma("sp", xi, src)
            for half in range(2):
                ps = nps()
                for k4 in range(4):
                    kc = half * 4 + k4
                    S.tr(ps[:, k4 * 128:(k4 + 1) * 128], xi[:, kc * 128:(kc + 1) * 128], ident)
                S.copy("dve" if half == 0 else "act", st[:, half * 4:half * 4 + 4, tb * 128:(tb + 1) * 128],
                       ps.re("p (k t) -> p k t", k=4))
        S.dma("sp", xt_view(g), st[:, :, :n])
    S.pop("S0")

    if stop_after == "S0":
        S.finish()
        return nc, S
    for l in range(L):
        last = (l == L - 1)
        S.push()
        wm = [S.sb("wm%d" % i, [128, 8, 512], F32) for i in range(4)]
        bm = S.sb("bm", [128, 48], F32)
        n1g = S.sb("n1g", [128, 8], F32)
        n2g = S.sb("n2g", [128, 8], F32)
        S.dma("sp", bm, b_modT[l])
        S.dma("sp", n1g, n1g_d[l])
        S.dma("sp", n2g, n2g_d[l])
        for nb in range(12):
            w = wm[nb % 4]
            S.dma("sp", w, w_mod[l, :, nb * 512:(nb + 1) * 512].re("(k p) n -> p k n", p=128))
            for jj in range(4):
                oc = nb * 4 + jj
                ps = nps()
                for kc in range(8):
                    S.mm(ps[:, 0:2], w[:, kc, jj * 128:(jj + 1) * 128], scv[:, kc, :], start=(kc == 0), stop=(kc == 7))
                S.ts("dve", modT[:, oc, :], ps[:, 0:2], bm[:, oc:oc + 1], None, op0=ALU.add)
        for jx in range(2):
            S.stt("dve", gsc[:, 0, :, jx], modT[:, 8:16, jx], 1.0, n1g, ALU.add, ALU.mult)
            S.stt("dve", gsc[:, 1, :, jx], modT[:, 32:40, jx], 1.0, n2g, ALU.add, ALU.mult)
        S.pop("S1")

        if stop_after == "S1":
            break
        S.push()
        alloc_wstg()
        Win = S.sb("Win", [128, 8, NA], BF16)
        load_w_kc(Win, w_inA[l], NA)
        Wuq = S.sb("Wuq", [128, 2, 2, 576], BF16)
        for c in range(2):
            load_w(Wuq[:, c, :, :].re("p v n -> p (v n)"), w_uq2[l, c * 128:(c + 1) * 128].re("p v n -> p (v n)"), 1152)
        Wkk = S.sb("Wkk", [128, 384], BF16)
        Wkv = S.sb("Wkv", [128, 384], BF16)
        load_w(Wkk, w_ukvk[l], 384)
        load_w(Wkv, w_ukvv[l], 384)
        gq = S.sb("gq", [128, 2], F32)
        gkv = S.sb("gkv", [128, 1], F32)
        dtb = S.sb("dtb", [128, 12], F32)
        S.dma("sp", gq, gq_d[l])
        S.dma("sp", gkv, gkv_d[l])
        S.dma("sp", dtb, dtb_d[l])
        xT2 = [S.sb("xT%d" % i, [128, 8, 512], F32) for i in range(2)]
        sqt = S.sb("sqt", [128, 8, 512], BF16)
        rstd = S.sb("rstd", [128, 512], F32)
        tmpf = [S.sb("tmpf%d" % i, [128, 512], F32) for i in range(4)]
        rp = [0]

        def rpair():
            rp[0] += 1
            return (tmpf[0], tmpf[1]) if rp[0] % 2 == 0 else (tmpf[2], tmpf[3])
        hT = S.sb("hT", [128, 8, 512], BF16)
        qaT = S.sb("qaT", [128, 2, 512], F32)
        kvaT = S.sb("kvaT", [128, 512], F32)
        krT = S.sb("krT", [32, 512], F32)
        krpT = S.sb("krpT", [32, 512], F32)
        ob16 = [S.sb("ob16_%d" % i, [128, 512], BF16) for i in range(3)]
        of32 = [S.sb("of32_%d" % i, [128, 512], F32) for i in range(2)]
        cqt = S.sb("cqt", [96, 512], F32)
        sqq = S.sb("sqq", [96, 512], F32)
        cst = S.sb("cst", [128, 512], F32)
        sst = S.sb("sst", [128, 512], F32)
        ckt = S.sb("ckt", [32, 512], F32)
        sqtk = S.sb("sqtk", [128, 1, 512], BF16)
        rstdk = S.sb("rstdk", [128, 512], F32)
        ktm = [tmpf[0][:32], tmpf[1][:32]]
        skt = S.sb("skt", [32, 512], F32)
        qan = S.sb("qan", [128, 2, 512], BF16)
        kvan = S.sb("kvan", [128, 512], BF16)
        krot = S.sb("krot", [32, 512], BF16)
        vs_t = [S.sb("vs_t%d" % i, [128, 2, 65], BF16) for i in range(2)]
        va_t = [S.sb("va_t%d" % i, [128, 6, 65], BF16) for i in range(2)]
        dt_t = [S.sb("dt_t%d" % i, [128, 4, 12], F32) for i in range(4)]
        for t_ in vs_t + va_t:
            S.memset("pool", t_, 1.0)
        cnt16 = [0]
        cnt32 = [0]

        def o16():
            cnt16[0] += 1
            return ob16[cnt16[0] % 3]

        def o32():
            cnt32[0] += 1
            return of32[cnt32[0] % 2]

        hT2 = [hT, S.sb("hTb", [128, 8, 512], BF16)]
        sqtN = [S.sb("sqtN", [128, 8, 512], BF16)] * 2
        rstdN = [S.sb("rstdN%d" % i, [128, 512], F32) for i in range(2)]
        ntf = [S.sb("ntf%d" % i, [128, 512], F32) for i in range(2)]

        def norm1(g_):
            t0_, n_, j_ = groups[g_]
            xT_ = xT2[g_ % 2]
            rms_rstd([xT_[:, kc, :n_] for kc in range(8)], n_, D, sqtN[g_ % 2], rstdN[g_ % 2])
            for kc in range(8):
                tf = ntf[kc % 2]
                S.stt("dve", tf[:, :n_], xT_[:, kc, :n_], gsc[:, 0, kc, j_:j_ + 1], rstdN[g_ % 2][:, :n_], ALU.mult, ALU.mult)
                S.act(hT2[g_ % 2][:, kc, :n_], tf[:, :n_], AF.Identity, bias=modT[:, kc, j_:j_ + 1])

        S.dma("sp", xT2[0][:, :, :groups[0][1]], xt_view(0))
        if NG > 1:
            S.dma("sp", xT2[1][:, :, :groups[1][1]], xt_view(1))
        norm1(0)
        for g, (t0, n, j) in enumerate(groups):
            hT = hT2[g % 2]
            if g + 1 < NG:
                norm1(g + 1)
            if g + 2 < NG:
                S.dma("sp", xT2[g % 2][:, :, :groups[g + 2][1]], xt_view(g + 2))
            S.dma("sp", cqt[:, :n], cq_d[:, t0:t0 + n])
            S.dma("sp", sqq[:, :n], sq_d[:, t0:t0 + n])
            S.dma("sp", cst[:, :n], cs_d[:, t0:t0 + n])
            S.dma("sp", sst[:, :n], ss_d[:, t0:t0 + n])
            S.dma("sp", ckt[:, :n], cq_d[64:96, t0:t0 + n])
            S.dma("sp", skt[:, :n], sq_d[64:96, t0:t0 + n])

            def fmchunk(c0, m):
                ps = nps()
                for kc in range(8):
                    S.mm(ps[:m, :n], Win[:, kc, c0:c0 + m], hT[:, kc, :n], start=(kc == 0), stop=(kc == 7))
                return ps

            for c in range(2):
                S.copy("act", qaT[:, c, :n], fmchunk(c * 128, 128)[:, :n])
            S.copy("act", kvaT[:, :n], fmchunk(256, 128)[:, :n])
            S.copy("dve", krT[:, :n], fmchunk(384, 32)[:32, :n])
            S.copy("dve", krpT[:, :n], fmchunk(416, 32)[:32, :n])
            for c in range(3):
                ob = o16()
                S.act(ob[:, :n], fmchunk(448 + c * 128, 128)[:, :n], AF.Silu)
                S.dma("sp", dv(ZT[c, :, t0:t0 + n], "ZT", g, c), ob[:, :n])
            rms_rstd([qaT[:, c, :n] for c in range(2)], n, 256, sqt, rstd)
            for c in range(2):
                S.stt("dve", qan[:, c, :n], qaT[:, c, :n], gq[:, c:c + 1], rstd[:, :n], ALU.mult, ALU.mult)
            rms_rstd([kvaT[:, :n]], n, 128, sqtk, rstdk)
            S.stt("dve", kvan[:, :n], kvaT[:, :n], gkv[:, 0:1], rstdk[:, :n], ALU.mult, ALU.mult)
            S.tt("dve", ktm[0][:, :n], krT[:, :n], ckt[:, :n], ALU.mult)
            S.tt("dve", ktm[1][:, :n], krpT[:, :n], skt[:, :n], ALU.mult)
            S.tt("pool", krot[:, :n], ktm[0][:, :n], ktm[1][:, :n], ALU.add)
            for h in range(6):
                S.dma("sp", dv(KT[h, 64:96, t0:t0 + n], "KTr", g, h), krot[:, :n])
            for c in range(7):
                of = o32()
                S.copy("dve" if c % 2 == 0 else "act", of[:, :n], fmchunk(832 + c * 128, 128)[:, :n])
                S.dma("sp", dv(XBC[c, :, t0:t0 + n], "XBC", g, c), of[:, :n])
            for kind, base, dst in (("q", 1728, QS), ("k", 2240, KS)):
                for c in range(2):
                    pa = fmchunk(base + c * 128, 128)
                    pb = fmchunk(base + 256 + c * 128, 128)
                    ta, tb_ = rpair()
                    S.tt("dve", ta[:, :n], pa[:, :n], cst[:, :n], ALU.mult)
                    S.tt("dve", tb_[:, :n], pb[:, :n], sst[:, :n], ALU.mult)
                    ob = o16()
                    S.tt("pool", ob[:, :n], ta[:, :n], tb_[:, :n], ALU.add)
                    S.dma("sp", dv(dst[c, :, t0:t0 + n], "S" + kind, g, c), ob[:, :n])
            for tb in range(n // 128):
                blk = t0 // 128 + tb
                ps = nps()
                for kc in range(8):
                    S.mm(ps[:, 0:140], hT[:, kc, tb * 128:(tb + 1) * 128], Win[:, kc, 2752:2892],
                         start=(kc == 0), stop=(kc == 7))
                vt = vs_t[blk % 2]
                S.copy("act", vt[:, :, 0:64], ps[:, 0:128].re("p (g d) -> p g d", g=2))
                S.dma("sp", dv(VS[blk], "VS", blk), vt.re("p g d -> p (g d)"))
                dtt = dt_t[blk % 4]
                S.tt("dve", dtt[:, 0, :], ps[:, 128:140], dtb, ALU.add)
                S.stt("dve", dtt[:, 1, :], dtt[:, 0, :], -1.0, dtt[:, 0, :], ALU.mult, ALU.max)
                S.act(dtt[:, 2, :], dtt[:, 1, :], AF.Exp, scale=-1.0)
                S.act(dtt[:, 2, :], dtt[:, 2, :], AF.Ln, bias=1.0)
                S.stt("dve", dtt[:, 3, :], dtt[:, 0, :], 0.0, dtt[:, 2, :], ALU.max, ALU.add)
                S.dma("sp", dv(DTD[blk], "DTD", blk), dtt[:, 3, :])
            for h in range(6):
                pa = nps()
                pb = nps()
                for c in range(2):
                    S.mm(pa[:96, :n], Wuq[:, c, 0, h * 96:(h + 1) * 96], qan[:, c, :n], start=(c == 0), stop=(c == 1))
                for c in range(2):
                    S.mm(pb[:96, :n], Wuq[:, c, 1, h * 96:(h + 1) * 96], qan[:, c, :n], start=(c == 0), stop=(c == 1))
                ta, tb_ = rpair()
                S.tt("dve", ta[:96, :n], pa[:96, :n], cqt[:, :n], ALU.mult)
                S.tt("dve", tb_[:96, :n], pb[:96, :n], sqq[:, :n], ALU.mult)
                ob = o16()
                S.tt("pool", ob[:96, :n], ta[:96, :n], tb_[:96, :n], ALU.add)
                S.dma("sp", dv(QT[h, :, t0:t0 + n], "QT", g, h), ob[:96, :n])
            for hp in range(3):
                ps = nps()
                S.mm(ps[:, :n], Wkk[:, hp * 128:(hp + 1) * 128], kvan[:, :n])
                ob = o16()
                S.copy("act", ob[:, :n], ps[:, :n])
                for e in range(2):
                    S.dma("sp", dv(KT[2 * hp + e, 0:64, t0:t0 + n], "KTn", g, 2 * hp + e),
                          ob[e * 64:(e + 1) * 64, :n])
            for tb in range(n // 128):
                blk = t0 // 128 + tb
                ps = nps()
                S.mm(ps[:, 0:384], kvan[:, tb * 128:(tb + 1) * 128], Wkv)
                vt = va_t[blk % 2]
                S.copy("dve", vt[:, :, 0:64], ps[:, 0:384].re("p (h d) -> p h d", h=6))
                S.dma("sp", dv(VA[blk], "VA", blk), vt.re("p h d -> p (h d)"))
        S.pop("S2")

        if stop_after == "S2":
            break
        S.push()
        KTs = S.sb("KTs", [96, 6, T], BF16)
        VAs = S.sb("VAs", [128, NB, 390], BF16)
        kres = [S.R("KTr", g, h) for g in range(NG) for h in range(6)] + \
               [S.R("KTn", g, h) for g in range(NG) for h in range(6)]
        vres = [S.R("VA", b) for b in range(NB)]
        nv = 4
        vb = [(i * NB) // nv for i in range(nv + 1)]
        S.dma("sp", KTs[:, 0, :], V(KT[0], kres))
        for i in range(nv):
            S.dma("sp", VAs[:, vb[i]:vb[i + 1], :], V(VA[vb[i]:vb[i + 1]].rearrange("b p f -> p b f"), vres))
        for h in range(1, 6):
            S.dma("sp", KTs[:, h, :], V(KT[h], kres))
        Qg = [S.sb("Qg%d" % i, [96, 6, 512], BF16) for i in range(2)]
        Pt = [S.sb("Pt%d" % i, [128, 512], BF16) for i in range(4)]
        Rr = S.sb("Rr", [128, 512], F32)
        rb = S.sb("rb", [64, 512], F32)
        yo = [S.sb("yo%d" % i, [64, 512], BF16) for i in range(2)]
        sc_mla = float(96 ** -0.5)
        LAG = 2
        qgroups = [g for g in range(NG) if not (last and groups[g][2] == 1)]

        def loadq(g):
            t0, n, j = groups[g]
            S.dma("sp", Qg[g % 2][:, :, :n],
                  V(QT[:, :, t0:t0 + n].rearrange("h p t -> p h t"), [S.R("QT", g, h) for h in range(6)]))

        loadq(qgroups[0])
        hcount = 0
        pend = []
        for qi, g in enumerate(qgroups):
            t0, n, j = groups[g]
            if qi + 1 < len(qgroups):
                loadq(qgroups[qi + 1])
            Q = Qg[g % 2]
            nkb = 2 if j == 1 else NB
            for h in range(6):
                O = PS[4 + hcount % 2]
                for i in range(nkb + LAG):
                    if i < nkb:
                        S.mm(PS[i % 4][:, :n], KTs[:, h, i * 128:(i + 1) * 128], Q[:, h, :n])
                        S.act(Pt[i % 4][:, :n], PS[i % 4][:, :n], AF.Exp, scale=sc_mla)
                    if i >= LAG:
                        kb = i - LAG
                        S.mm(O[:65, :n], VAs[:, kb, h * 65:(h + 1) * 65], Pt[kb % 4][:, :n],
                             start=(kb == 0), stop=(kb == nkb - 1))
                    if i == (10 if nkb > 12 else 3) and pend:
                        pend.pop(0)()

                def epi(O=O, n=n, t0=t0, g=g, h=h, y=yo[hcount % 2]):
                    S.recip(Rr[64:65, :n], O[64:65, :n])
                    S.mm(PS[6][:64, :n], onesf[64:65, 0:64], Rr[64:65, :n])
                    S.copy("dve", rb[:, :n], PS[6][:64, :n])
                    S.tt("dve", y[:, :n], O[0:64, :n], rb[:, :n], ALU.mult)
                    S.dma("sp", dv(YT[h // 2, (h % 2) * 64:(h % 2) * 64 + 64, t0:t0 + n], "YTa", g, h), y[:, :n])

                pend.append(epi)
                hcount += 1
        while pend:
            pend.pop(0)()
        S.pop("S3")

        if stop_after == "S3":
            break
        S.push()
        QSs = S.sb("QSs", [128, 2, T], BF16)
        KSs = S.sb("KSs", [128, 2, T], BF16)
        VSs = S.sb("VSs", [128, NB, 130], BF16)
        S.dma("sp", QSs, V(QS.rearrange("c p t -> p c t"), [S.R("Sq", g, c) for g in range(NG) for c in range(2)]))
        S.dma("sp", KSs, V(KS.rearrange("c p t -> p c t"), [S.R("Sk", g, c) for g in range(NG) for c in range(2)]))
        S.dma("sp", VSs, V(VS.rearrange("b p f -> p b f"), [S.R("VS", b) for b in range(NB)]))
        Qp = [S.sb("Qp%d" % r, [128, 2, T], BF16) for r in range(2)]
        for r in range(2):
            S.memset("pool", Qp[r][(1 - r) * 64:(2 - r) * 64], 0.0)
            S.copy("dve" if r == 0 else "act", Qp[r][r * 64:(r + 1) * 64], QSs[r * 64:(r + 1) * 64])
        es = S.sb("es", [128, 4], F32)
        S.dma("sp", es, sink_d[l])
        S.act(es, es, AF.Exp)
        Pw = [S.sb("Pw%d" % i, [128, 6, 256], BF16) for i in range(2)]
        mlo2 = S.sb("mlo2", [128, 256], BF16)
        mhi2 = S.sb("mhi2", [128, 256], BF16)
        for r in range(2):
            S.copy("dve", mlo2[:, r * 128:(r + 1) * 128], lmatb)
            S.copy("dve", mhi2[:, r * 128:(r + 1) * 128], umatb)
        Rw = S.sb("Rw", [128, 256], F32)
        rbw = S.sb("rbw", [64, 256], F32)
        yw = [S.sb("yw%d" % i, [64, 256], BF16) for i in range(2)]
        it = 0
        pend4 = []
        for qb in range(NB):
            if last and qb < 2:
                continue
            if qb < 2:
                keys = [(0, None), (1, None)]
            else:
                keys = [(0, None), (1, None)]
                if qb - 1 >= 2:
                    keys.append((qb - 1, "lo"))
                keys.append((qb, None))
                if qb + 1 < NB:
                    keys.append((qb + 1, "hi"))
            for gg in range(2):
                P = Pw[it % 2]
                tiles = [PS[(3 * it + q_) % 5] for q_ in range(3)]
                O = PS[5 + it % 2]
                ntile = (len(keys) + 1) // 2
                for idx, (kb, mk) in enumerate(keys):
                    pst = tiles[idx // 2]
                    for r in range(2):
                        S.mm(pst[:, (idx % 2) * 256 + r * 128:(idx % 2) * 256 + (r + 1) * 128],
                             KSs[:, gg, kb * 128:(kb + 1) * 128],
                             Qp[r][:, gg, qb * 128:(qb + 1) * 128])
                for ti in range(ntile):
                    w = min(2, len(keys) - 2 * ti) * 256
                    S.act(P[:, 2 * ti:2 * ti + 2, :].re("p a b -> p (a b)")[:, :w], tiles[ti][:, :w],
                          AF.Exp, scale=0.125)
                for idx, (kb, mk) in enumerate(keys):
                    if mk is not None:
                        m = mlo2 if mk == "lo" else mhi2
                        S.tt("dve", P[:, idx, :], P[:, idx, :], m, ALU.mult)
                for idx, (kb, mk) in enumerate(keys):
                    S.mm(O[:65, :256], VSs[:, kb, gg * 65:(gg + 1) * 65], P[:, idx, :],
                         start=(idx == 0), stop=(idx == len(keys) - 1))

                def epi4(O=O, gg=gg, qb=qb, pbc=tiles[2], y=yw[it % 2]):
                    for r in range(2):
                        S.ts("dve", Rw[64:65, r * 128:(r + 1) * 128], O[64:65, r * 128:(r + 1) * 128],
                             es[64:65, 2 * gg + r:2 * gg + r + 1], None, op0=ALU.add)
                    S.act(Rw[64:65, :], Rw[64:65, :], AF.Ln)
                    S.act(Rw[64:65, :], Rw[64:65, :], AF.Exp, scale=-1.0)
                    S.mm(pbc[:64, 256:512], onesf[64:65, 0:64], Rw[64:65, :])
                    S.copy("dve", rbw, pbc[:64, 256:512])
                    S.tt("dve", y, O[0:64, :256], rbw, ALU.mult)
                    for r in range(2):
                        S.dma("sp", dv(YT[6 + gg, r * 64:(r + 1) * 64, qb * 128:(qb + 1) * 128], "YTc", qb, gg, r),
                              y[:, r * 128:(r + 1) * 128])

                if pend4:
                    pend4.pop(0)()
                pend4.append(epi4)
                it += 1
        while pend4:
            pend4.pop(0)()
        S.pop("S4")

        if stop_after == "S4":
            break
        S.push()
        xbcs = S.sb("xbcs", [128, 7, T], BF16)
        cw = S.sb("cw", [128, 7, 3], F32)
        cb = S.sb("cb", [128, 7], F32)
        S.dma("sp", cw, cw_d[l])
        S.dma("sp", cb, cb_d[l])
        cin = [S.sb("cin%d" % i, [128, 514], F32) for i in range(2)]
        cu = [S.sb("cu%d" % i, [128, 512], F32) for i in range(2)]
        dts = S.sb("dts", [128, NB, 12], F32)
        S.dma("sp", dts, V(DTD.rearrange("b p f -> p b f"), [S.R("DTD", b) for b in range(NB)]))
        Aa = S.sb("Aa", [128, 12], F32)
        S.dma("sp", Aa, alog_d[l])
        S.act(Aa, Aa, AF.Exp)
        S.ts("dve", Aa, Aa, -1.0, None, op0=ALU.mult)
        dta = S.sb("dta", [128, NB, 12], F32)
        S.tt("dve", dta, dts, V(Aa.ap.rearrange("p (o f) -> p o f", o=1).to_broadcast([128, NB, 12]), Aa.res), ALU.mult)
        cum = S.sb("cum", [128, NB, 24], F32)
        for blk in range(NB):
            ps = nps()
            S.mm(ps[:, 0:6], umat, dta[:, blk, 0:6])
            S.mm(ps[:, 6:12], lmat, dta[:, blk, 6:12])
            S.mm(ps[:, 12:24], onesf, dta[:, blk, 0:12])
            S.copy("act", cum[:, blk, :], ps[:, 0:24])
        wdec = S.sb("wdec", [128, NB, 12], F32)
        etot = S.sb("etot", [128, NB, 12], F32)
        dtw = S.sb("dtw", [128, NB, 12], F32)
        S.tt("pool", wdec, cum[:, :, 12:24], cum[:, :, 0:12], ALU.subtract)
        S.act(wdec, wdec, AF.Exp)
        S.act(etot, cum[:, :, 12:24], AF.Exp)
        S.tt("pool", dtw, dts, wdec, ALU.mult)
        xs_tok = S.sb("xs_tok", [128, NB, 384], BF16)
        b_tok = S.sb("b_tok", [128, NB, 256], BF16)
        ci = 0
        for g, (t0, n, j) in enumerate(groups):
            for c in range(7):
                s0, s1 = (0, CTX) if j == 1 else (CTX, T)
                ct = cin[ci % 2]
                u = cu[ci % 2]
                ci += 1
                lo = max(t0 - 1, s0)
                hi = min(t0 + n + 1, s1)
                if lo == t0:
                    S.memset("pool", ct[:, 0:1], 0.0)
                if hi == t0 + n:
                    S.memset("pool", ct[:, n + 1:n + 2], 0.0)
                S.dma("sp", ct[:, lo - (t0 - 1):hi - (t0 - 1)],
                      V(XBC[c, :, lo:hi], [S.R("XBC", gx, c) for gx in range(NG)]))
                S.ts("dve", u[:, :n], ct[:, 0:n], cw[:, c, 0:1], None, op0=ALU.mult)
                S.stt("dve", u[:, :n], ct[:, 1:n + 1], cw[:, c, 1:2], u[:, :n], ALU.mult, ALU.add)
                S.stt("dve", u[:, :n], ct[:, 2:n + 2], cw[:, c, 2:3], u[:, :n], ALU.mult, ALU.add)
                S.act(xbcs[:, c, t0:t0 + n], u[:, :n], AF.Silu, bias=cb[:, c:c + 1])
            for blk in range(t0 // 128, (t0 + n) // 128):
                for c in range(5):
                    S.tr(PSB[:, c * 128:(c + 1) * 128], xbcs[:, c, blk * 128:(blk + 1) * 128], identb)
                S.copy("dve", xs_tok[:, blk, :], PSB[:, 0:384])
                S.copy("act", b_tok[:, blk, :], PSB[:, 384:640])
        dcol = S.sb("dcol", [128, 3], F32)
        sng = S.sb("sng", [128, 3], F32)
        S.dma("sp", dcol, dcol_d[l])
        S.dma("sp", sng, sng_d[l])

        xdtp = [[S.sb("xdtp%d_%d" % (d_, i), [128, 3, 2, 2, 64], BF16) for i in range(2)] for d_ in range(2)]
        xdtw = [[S.sb("xdtw%d_%d" % (d_, i), [128, 6, 64], BF16) for i in range(2)] for d_ in range(2)]
        Hs = [S.sb("Hs%d" % d_, [128, 6, 64], F32) for d_ in range(2)]
        Hbp = [S.sb("Hbp%d" % d_, [128, 3, 2, 2, 64], BF16) for d_ in range(2)]
        cbm = [[S.sb("cbm%d_%d" % (d_, i), [128, 2, 128], BF16) for i in range(2)] for d_ in range(2)]
        NK = 12
        blt = [S.sb("blt%d" % i, [128, 128], F32) for i in range(NK)]
        tdf = [S.sb("tdf%d" % i, [128, 128], F32) for i in range(NK)]
        Mt = [S.sb("Mt%d" % i, [128, 128], BF16) for i in range(NK)]
        e2 = [S.sb("e2_%d" % i, [128, 128], F32) for i in range(NK)]
        Cd = [S.sb("Cd%d" % i, [128, 128], BF16) for i in range(NK)]
        yst = [[S.sb("yst%d_%d" % (d_, i), [128, 3, 128], F32) for i in range(2)] for d_ in range(2)]
        ysum = [S.sb("ysum%d" % i, [128, 3, 512], F32) for i in range(2)]
        yfl = [[S.sb("yfl%d_%d" % (d_, i), [128, 3, 128], F32) for i in range(2)] for d_ in range(2)]
        zs = S.sb("zs", [128, 3, 512], BF16)
        y3 = S.sb("y3", [128, 3, 512], F32)
        sq3 = S.sb("sq3", [128, 3, 512], BF16)
        rstd5 = S.sb("rstd5", [128, 512], F32)
        yob = S.sb("yob", [128, 3, 512], BF16)
        for d_ in range(2):
            for t_ in xdtp[d_]:
                S.memset("pool", t_, 0.0)
            S.memset("pool", Hs[d_], 0.0)
            S.memset("pool", Hbp[d_], 0.0)
        blk2g = {}
        for g, (t0, n, j) in enumerate(groups):
            for tb in range(n // 128):
                blk2g[t0 // 128 + tb] = g
        kcount = 0
        orders = [list(range(NB)), [1, 0] + list(range(NB - 1, 1, -1))]
        PSY = [PS[6], PS[5]]
        nrot[0] = 5
        seen_blk = set()
        gcount = {}
        kc_ = [0]

        def unit_pro(d, step):
            blk = orders[d][step]
            mask = umatb if d == 0 else lmatb
            tsl = slice(blk * 128, (blk + 1) * 128)
            xp = xdtp[d][step % 2]
            xw = xdtw[d][step % 2]
            xs4 = xs_tok[:, blk, :].re("p (a e f) -> p a e f", a=3, e=2)
            for e in range(2):
                dsel = V(dts.ap[:, blk, d * 6:(d + 1) * 6].rearrange("p (a e o) -> p a e o", a=3, e=2)[:, :, e, :]
                         .to_broadcast([128, 3, 64]), dts.res)
                S.tt("pool", xp[:, :, e, e, :], xs4[:, :, e, :], dsel, ALU.mult)
            S.tt("pool", xw, xs_tok[:, blk, :].re("p (h f) -> p h f", h=6),
                 V(dtw.ap[:, blk, d * 6:(d + 1) * 6].rearrange("p (h o) -> p h o", o=1).to_broadcast([128, 6, 64]),
                   dtw.res), ALU.mult)
            cbt = cbm[d][step % 2]
            for gg in range(2):
                cps = PS[3][:, (2 * d + gg) * 128:(2 * d + gg + 1) * 128]
                S.mm(cps, xbcs[:, 3 + gg, tsl], xbcs[:, 5 + gg, tsl])
                S.tt("dve", cbt[:, gg, :], cps, mask, ALU.mult)

        def PR(k):
            return PS[k // 4][:, (k % 4) * 128:(k % 4 + 1) * 128]

        def ph_A(d, step, hh):
            blk = orders[d][step]
            k = d * 6 + hh
            S.act(blt[k], onesf, AF.Identity, scale=dta[:, blk, k:k + 1])
            S.mm(PR(k), blt[k], umat if d == 0 else lmat)

        def ph_B1(d, step, hh):
            blk = orders[d][step]
            k = d * 6 + hh
            S.ts("dve", tdf[k], PR(k), cum[:, blk, k:k + 1], 0.0, op0=ALU.subtract, op1=ALU.min)

        def ph_B2(d, step, hh):
            k = d * 6 + hh
            S.act(tdf[k], tdf[k], AF.Exp)
            S.act(e2[k], PR(k), AF.Exp)

        def ph_B3(d, step, hh):
            blk = orders[d][step]
            tsl = slice(blk * 128, (blk + 1) * 128)
            k = d * 6 + hh
            gg = hh // 3
            S.tt("dve", Mt[k], tdf[k], cbm[d][step % 2][:, gg, :], ALU.mult)
            S.tt("pool", Cd[k], xbcs[:, 5 + gg, tsl], e2[k], ALU.mult)

        def ph_B4(d, step, hh):
            k = d * 6 + hh
            xp = xdtp[d][step % 2]
            psY = PSY[d]
            pr, e = hh // 2, hh % 2
            S.mm(psY[:, pr * 128:(pr + 1) * 128], xp[:, pr, e, :, :].re("p a f -> p (a f)"), Mt[k],
                 start=(e == 0), stop=False)
            S.mm(psY[:, pr * 128:(pr + 1) * 128], Hbp[d][:, pr, e, :, :].re("p a f -> p (a f)"), Cd[k],
                 start=False, stop=(e == 1))

        def unit_epi(d, step):
            blk = orders[d][step]
            tsl = slice(blk * 128, (blk + 1) * 128)
            xw = xdtw[d][step % 2]
            psY = PSY[d]
            psS = PS[4]
            for gg in range(2):
                S.mm(psS[:, gg * 192:(gg + 1) * 192], b_tok[:, blk, gg * 128:(gg + 1) * 128],
                     xw[:, 3 * gg:3 * gg + 3, :].re("p h f -> p (h f)"))
            S.tt("dve", Hs[d], Hs[d], V(etot.ap[:, blk, d * 6:(d + 1) * 6].rearrange("p (h o) -> p h o", o=1)
                                        .to_broadcast([128, 6, 64]), etot.res), ALU.mult)
            S.tt("dve", Hs[d], Hs[d], psS[:, 0:384].re("p (h f) -> p h f", h=6), ALU.add)
            Hs4 = Hs[d].re("p (a e) f -> p a e f", a=3)
            for e in range(2):
                S.copy("act", Hbp[d][:, :, e, e, :], Hs4[:, :, e, :])
            g = blk2g[blk]
            t0, n, j = groups[g]
            if blk not in seen_blk:
                seen_blk.add(blk)
                ys = yst[d][step % 2]
                S.copy("act", ys, psY[:, 0:384].re("p (a q) -> p a q", a=3))
                S.dma("sp", dv(YF[:, :, tsl].rearrange("c p t -> p c t"), "YF", blk), ys)
            else:
                yf = yfl[d][step % 2]
                S.dma("sp", yf, dv(YF[:, :, tsl].rearrange("c p t -> p c t"), "YF", blk))
                off = blk * 128 - t0
                ysg = ysum[g % 2]
                S.tt("dve", ysg[:, :, off:off + 128], psY[:, 0:384].re("p (a q) -> p a q", a=3), yf, ALU.add)
                gcount[g] = gcount.get(g, 0) + 1
                if gcount[g] == n // 128 and not (last and j == 1):
                    S.dma("sp", zs[:, :, :n], V(ZT[:, :, t0:t0 + n].rearrange("c p t -> p c t"),
                                                [S.R("ZT", g, c) for c in range(3)]))
                    for c in range(3):
                        S.stt("dve", y3[:, c, :n], xbcs[:, c, t0:t0 + n], dcol[:, c:c + 1], ysg[:, c, :n],
                              ALU.mult, ALU.add)
                    S.tt("dve", y3[:, :, :n], y3[:, :, :n], zs[:, :, :n], ALU.mult)
                    rms_rstd([y3[:, c, :n] for c in range(3)], n, 384, sq3, rstd5)
                    for c in range(3):
                        S.stt("dve", yob[:, c, :n], y3[:, c, :n], sng[:, c:c + 1], rstd5[:, :n], ALU.mult, ALU.mult)
                    S.dma("sp", dv(YT[3:6, :, t0:t0 + n].rearrange("c p t -> p c t"), "YTb", g), yob[:, :, :n])

        for step in range(NB):
            unit_pro(0, step)
            unit_pro(1, step)
            for ph in (ph_A, ph_B1, ph_B2, ph_B3):
                for hh in range(6):
                    for d in range(2):
                        ph(d, step, hh)
            for d in range(2):
                for hh in range(6):
                    ph_B4(d, step, hh)
            unit_epi(0, step)
            unit_epi(1, step)
        nrot[0] = 6
        S.pop("S5")

        if stop_after == "S5":
            break
        S.push()
        alloc_wstg()
        Wo = S.sb("Wo", [128, 8, D], BF16)
        for kc in range(8):
            load_w(Wo[:, kc, :], w_out[l, kc * 128:(kc + 1) * 128, :], D)
        yT2 = [S.sb("yT%d" % i, [128, 8, 512], BF16) for i in range(2)]
        xT2 = [S.sb("xTb%d" % i, [128, 8, 512], F32) for i in range(2)]
        sqt = S.sb("sqt6", [128, 8, 512], BF16)
        rstd = S.sb("rstd6", [128, 512], F32)
        tmpf = [S.sb("tmpf6_%d" % i, [128, 512], F32) for i in range(2)]
        h2T = [S.sb("h2T%d" % i, [128, 8, 512], BF16) for i in range(2)]
        fgroups = [g for g in range(NG) if not (last and groups[g][2] == 1)]

        def yt_view(g):
            t0, n, j = groups[g]
            res = [S.R("YTa", g, h) for h in range(6)] + [S.R("YTb", g)] + \
                  [S.R("YTc", qb, gg, r) for qb in range(t0 // 128, (t0 + n) // 128) for gg in range(2) for r in range(2)]
            return V(YT[:, :, t0:t0 + n].rearrange("k p t -> p k t"), res)

        def load6(g):
            n = groups[g][1]
            S.dma("sp", yT2[g % 2][:, :, :n], yt_view(g))
            S.dma("sp", xT2[g % 2][:, :, :n], xt_view(g))

        load6(fgroups[0])
        for fi, g in enumerate(fgroups):
            t0, n, j = groups[g]
            if fi + 1 < len(fgroups):
                load6(fgroups[fi + 1])
            yT, xT, hh2 = yT2[g % 2], xT2[g % 2], h2T[g % 2]
            for oc in range(8):
                ps = nps()
                for kc in range(8):
                    S.mm(ps[:, :n], Wo[:, kc, oc * 128:(oc + 1) * 128], yT[:, kc, :n], start=(kc == 0), stop=(kc == 7))
                S.stt("dve", xT[:, oc, :n], ps[:, :n], modT[:, 16 + oc, j:j + 1], xT[:, oc, :n], ALU.mult, ALU.add)
            S.dma("sp", xt_view(g), xT[:, :, :n])
            rms_rstd([xT[:, kc, :n] for kc in range(8)], n, D, sqt, rstd)
            for kc in range(8):
                tf = tmpf[kc % 2]
                S.stt("dve", tf[:, :n], xT[:, kc, :n], gsc[:, 1, kc, j:j + 1], rstd[:, :n], ALU.mult, ALU.mult)
                S.act(hh2[:, kc, :n], tf[:, :n], AF.Identity, bias=modT[:, 24 + kc, j:j + 1])
            S.dma("sp", dv(H2[:, :, t0:t0 + n].rearrange("k p t -> p k t"), "H2", g), hh2[:, :, :n])
        S.pop("S6a")

        if stop_after == "S6a":
            break
        S.push()
        alloc_wstg()
        Wu = S.sb("Wu", [128, 8, 2 * FFN], BF16)
        load_w_kc(Wu, w_up[l], 2 * FFN)
        h2l = [S.sb("h2l%d" % i, [128, 8, 512], BF16) for i in range(2)]
        gst = [S.sb("gst%d" % i, [128, 4, 512], BF16) for i in range(3)]

        def loadh(g):
            t0, n, j = groups[g]
            S.dma("sp", h2l[g % 2][:, :, :n], dv(H2[:, :, t0:t0 + n].rearrange("k p t -> p k t"), "H2", g))

        loadh(fgroups[0])
        sc_ = 0
        for fi, g in enumerate(fgroups):
            t0, n, j = groups[g]
            if fi + 1 < len(fgroups):
                loadh(fgroups[fi + 1])
            hh2 = h2l[g % 2]
            for u4 in range(2 * FC // 4):
                st = gst[sc_ % 3]
                sc_ += 1
                for u in range(4):
                    uc = u4 * 4 + u
                    ps = nps()
                    for kc in range(8):
                        S.mm(ps[:, :n], Wu[:, kc, uc * 128:(uc + 1) * 128], hh2[:, kc, :n], start=(kc == 0), stop=(kc == 7))
                    S.copy("act" if u % 2 == 0 else "dve", st[:, u, :n], ps[:, :n])
                S.dma("sp", dv(GV[u4 * 4:u4 * 4 + 4, :, t0:t0 + n].rearrange("c p t -> p c t"), "GV", g, u4), st[:, :, :n])
        S.pop("S6b")

        if stop_after == "S6b":
            break
        S.push()
        alloc_wstg()
        Wd = S.sb("Wd", [128, FC, D], BF16)
        for c in range(FC):
            load_w(Wd[:, c, :], w_dn[l, c * 128:(c + 1) * 128, :], D)
        fcw = S.sb("fcw", [128, FC, 3], F32)
        fcb = S.sb("fcb", [128, FC], F32)
        S.dma("sp", fcw, fcw_d[l])
        S.dma("sp", fcb, fcb_d[l])
        gat2 = [S.sb("gat%d" % i, [128, FC, 514], BF16) for i in range(2)]
        val2 = [S.sb("val%d" % i, [128, FC, 512], BF16) for i in range(2)]
        xT7 = S.sb("xT7", [128, 8, 512], F32)
        su = [S.sb("su%d" % i, [128, 512], BF16) for i in range(2)]
        gvres = [S.R("GV", g, u4) for g in range(NG) for u4 in range(2 * FC // 4)]
        dg = S.sb("dg", [128, FC, 3, 128], BF16)
        for c in range(FC):
            for jt in range(3):
                S.ts("dve" if (c + jt) % 2 == 0 else "pool", dg[:, c, jt, :], identb, fcw[:, c, jt:jt + 1], None, op0=ALU.mult)

        def load7(fi):
            g = fgroups[fi]
            t0, n, j = groups[g]
            s0, s1 = (0, CTX) if j == 1 else (CTX, T)
            gat, val = gat2[fi % 2], val2[fi % 2]
            lo = max(t0 - 1, s0)
            hi = min(t0 + n + 1, s1)
            if lo == t0:
                S.memset("pool", gat[:, :, 0:1], 0.0)
            if hi == t0 + n:
                S.memset("pool", gat[:, :, n + 1:n + 2], 0.0)
            S.dma("sp", gat[:, :, lo - (t0 - 1):hi - (t0 - 1)], V(GV[FC:2 * FC, :, lo:hi].rearrange("c p t -> p c t"), gvres))
            S.dma("sp", val[:, :, :n], V(GV[0:FC, :, t0:t0 + n].rearrange("c p t -> p c t"), gvres))

        load7(0)
        for fi, g in enumerate(fgroups):
            t0, n, j = groups[g]
            gat, val = gat2[fi % 2], val2[fi % 2]
            xT = xT7
            S.dma("sp", xT[:, :, :n], xt_view(g))
            if fi + 1 < len(fgroups):
                load7(fi + 1)
            for c in range(FC):
                s_ = su[c % 2]
                psc = nps()
                for jt in range(3):
                    S.mm(psc[:, :n], dg[:, c, jt, :], gat[:, c, jt:jt + n], start=(jt == 0), stop=(jt == 2))
                S.act(s_[:, :n], psc[:, :n], AF.Silu, bias=fcb[:, c:c + 1])
                S.tt("dve", val[:, c, :n], s_[:, :n], val[:, c, :n], ALU.mult)
            for oc in range(8):
                ps = nps()
                for c in range(FC):
                    S.mm(ps[:, :n], Wd[:, c, oc * 128:(oc + 1) * 128], val[:, c, :n], start=(c == 0), stop=(c == FC - 1))
                S.stt("dve", xT[:, oc, :n], ps[:, :n], modT[:, 40 + oc, j:j + 1], xT[:, oc, :n], ALU.mult, ALU.add)
            S.dma("sp", xt_view(g), xT[:, :, :n])
        S.pop("S7")

    if stop_after is not None:
        S.finish()
        return nc, S
    S.push()
    fng = S.sb("fng", [128, 8], F32)
    S.dma("sp", fng, fng_d)
    xT2 = [S.sb("xT8_%d" % i, [128, 8, 512], F32) for i in range(2)]
    sqt = S.sb("sqt8", [128, 8, 512], BF16)
    rstd = S.sb("rstd8", [128, 512], F32)
    xn = S.sb("xn", [128, 8, 512], F32)
    ot = [S.sb("ot%d" % i, [128, D], F32) for i in range(2)]
    oi = 0
    for g, (t0, n, j) in enumerate(groups):
        if j == 1:
            continue
        xT = xT2[g % 2]
        S.dma("sp", xT[:, :, :n], xt_view(g))
        rms_rstd([xT[:, kc, :n] for kc in range(8)], n, D, sqt, rstd)
        for kc in range(8):
            S.stt("dve", xn[:, kc, :n], xT[:, kc, :n], fng[:, kc:kc + 1], rstd[:, :n], ALU.mult, ALU.mult)
        for tb in range(n // 128):
            o = ot[oi % 2]
            oi += 1
            for half in range(2):
                ps = nps()
                for k4 in range(4):
                    kc = half * 4 + k4
                    S.tr(ps[:, k4 * 128:(k4 + 1) * 128], xn[:, kc, tb * 128:(tb + 1) * 128], ident)
                S.copy("dve" if half == 0 else "act", o[:, half * 512:(half + 1) * 512], ps)
            r0 = t0 - CTX + tb * 128
            S.dma("sp", dv(out_d[r0:r0 + 128, :], "OUT", r0), o)
    S.pop("S8")
    S.finish()
    return nc, S


_CACHE = {}


def kernel(**inputs):
    nlat = inputs["x"].shape[1]
    depth = inputs["w_mod"].shape[0]
    ncores = inputs["x"].shape[0]
    maps = prep_inputs(inputs, nlat, depth, ncores)
    nc, S = build_program(nlat, depth)
    res = run_bass_kernel_spmd(nc, maps, core_ids=list(range(ncores)))
    out = np.stack([np.asarray(res.results[b]["out"], np.float32) for b in range(ncores)], axis=0)
    return out
```

```python
import contextlib
import math
import numpy as np
import concourse.bass as bass
import concourse.mybir as mybir
from concourse.bass_utils import run_bass_kernel_spmd

F32 = mybir.dt.float32
BF16 = mybir.dt.bfloat16
AF = mybir.ActivationFunctionType
ALU = mybir.AluOpType

D = 1024
KC = 8
CTX = 256
EPS = 1e-6
NA = 2892
FFN = 2816
FC = 22


class Res:
    __slots__ = ("name", "w_eng", "w_dma", "r_eng", "r_dma", "excl", "reg", "hist")
    ALL = []

    def __init__(self, name="", excl=False, reg=False):
        self.name = name
        self.excl = excl
        self.reg = reg
        self.clear()
        Res.ALL.append(self)

    def clear(self):
        self.w_eng = {}
        self.w_dma = set()
        self.r_eng = {}
        self.r_dma = set()
        self.hist = []


def _bbox(ap):
    dims = ap.ap
    off = ap.offset
    pstep, pcnt = dims[0]
    if pstep <= 0:
        return (0, 128, 0, 1 << 30)
    p0 = off // pstep
    f0 = off % pstep
    span = 1
    for st, c in dims[1:]:
        span += (c - 1) * abs(st)
    return (p0, p0 + pcnt, f0, f0 + span)


class V:
    __slots__ = ("ap", "res")

    def __init__(self, ap, res):
        self.ap = ap
        self.res = res if isinstance(res, (list, tuple)) else [res]

    def __getitem__(self, idx):
        return V(self.ap[idx], self.res)

    def re(self, pattern, **kw):
        return V(self.ap.rearrange(pattern, **kw), self.res)

    def bc(self, shape):
        return V(self.ap.to_broadcast(list(shape)), self.res)


NDMA = 48
CENG = ("pe", "dve", "act", "pool")


class Sched:
    def __init__(self, nc):
        self.nc = nc
        self.es = contextlib.ExitStack()
        self.scopes = []
        self.ops = []
        self.seq = {e: 0 for e in CENG}
        self.sig = {e: 0 for e in CENG}
        self.ndma = 0
        self.eng = {"pe": nc.tensor, "dve": nc.vector, "act": nc.scalar, "pool": nc.gpsimd, "sp": nc.sync}
        self.sems = {e: self.es.enter_context(nc.semaphore("sem_" + e)) for e in CENG}
        self.dsems = [self.es.enter_context(nc.semaphore("dsem%d" % i)) for i in range(NDMA)]
        self.bar = self.es.enter_context(nc.semaphore("bar"))
        self.nbar = 0
        self.seen_eng = {e: {} for e in self.eng}
        self.seen_dma = {e: set() for e in self.eng}
        self.dma_done_upto = 0
        self.rkeys = {}
        self.nwait = 0
        self.nops = 0
        self.marks = []
        self.dummy = None

    def _stack(self):
        return self.scopes[-1] if self.scopes else self.es

    def sb(self, name, shape, dt):
        t = self._stack().enter_context(self.nc.sbuf_tensor(name + "_%d" % len(Res.ALL), list(shape), dt))
        return V(t[:], Res(name, reg=True))

    def ps(self, name, shape, dt=F32):
        t = self._stack().enter_context(self.nc.psum_tensor(name, list(shape), dt))
        return V(t[:], Res(name, excl=True))

    def R(self, *key):
        r = self.rkeys.get(key)
        if r is None:
            r = Res(str(key))
            self.rkeys[key] = r
        return r

    def push(self):
        self.scopes.append(contextlib.ExitStack())

    def pop(self, tag=""):
        self.flush()
        self.scopes.pop().close()
        self.marks.append((tag, dict(self.seq), self.ndma))

    def _rec(self, eng, fn, reads, writes, dma=False):
        deps_eng = {}
        deps_dma = set()

        def add_eng(d):
            for e, s in d.items():
                if s > deps_eng.get(e, 0):
                    deps_eng[e] = s

        def add_who(who):
            if who[0] == "dma":
                deps_dma.add(who[1])
            elif who[2] > deps_eng.get(who[1], 0):
                deps_eng[who[1]] = who[2]

        rres = []
        wres = []
        rreg = []
        wreg = []
        for v in reads:
            if isinstance(v, V):
                for r in v.res:
                    if r.reg:
                        rreg.append((r, _bbox(v.ap)))
                    else:
                        (wres if r.excl else rres).append(r)
        for v in writes:
            for r in v.res:
                if r.reg:
                    wreg.append((r, _bbox(v.ap)))
                else:
                    wres.append(r)
        for r in rres:
            add_eng(r.w_eng)
            deps_dma |= r.w_dma
        for r in wres:
            add_eng(r.w_eng)
            deps_dma |= r.w_dma
            add_eng(r.r_eng)
            deps_dma |= r.r_dma
        for r, bx in rreg:
            for (p0, p1, f0, f1, kind, who) in r.hist:
                if kind == "w" and p0 < bx[1] and bx[0] < p1 and f0 < bx[3] and bx[2] < f1:
                    add_who(who)
        for r, bx in wreg:
            for (p0, p1, f0, f1, kind, who) in r.hist:
                if p0 < bx[1] and bx[0] < p1 and f0 < bx[3] and bx[2] < f1:
                    add_who(who)
        if dma:
            did = self.ndma
            self.ndma += 1
            if did >= NDMA:
                deps_dma.add(did - NDMA)
            me = ("dma", did)
            who_me = ("dma", did)
        else:
            self.seq[eng] += 1
            me = (eng, self.seq[eng])
            who_me = ("eng", eng, self.seq[eng])
            if eng == "pe":
                deps_eng.pop("pe", None)
        for r in wres:
            r.clear()
            if dma:
                r.w_dma.add(me[1])
            else:
                r.w_eng[eng] = me[1]
        for r in rres:
            if r in wres:
                continue
            if dma:
                r.r_dma.add(me[1])
            else:
                if me[1] > r.r_eng.get(eng, 0):
                    r.r_eng[eng] = me[1]
        for r, bx in wreg:
            r.hist = [h for h in r.hist if not (bx[0] <= h[0] and h[1] <= bx[1] and bx[2] <= h[2] and h[3] <= bx[3])]
            r.hist.append((bx[0], bx[1], bx[2], bx[3], "w", who_me))
        for r, bx in rreg:
            if not dma:
                r.hist = [h for h in r.hist if not (h[4] == "r" and h[5][0] == "eng" and h[5][1] == eng
                                                    and h[0] == bx[0] and h[1] == bx[1] and h[2] == bx[2] and h[3] == bx[3])]
            r.hist.append((bx[0], bx[1], bx[2], bx[3], "r", who_me))
        self.ops.append((eng, fn, deps_eng, deps_dma, me))

    def flush(self, final=False):
        nc = self.nc
        need = {e: set() for e in CENG}
        last = {}
        for eng, fn, deps_eng, deps_dma, me in self.ops:
            for e, s in deps_eng.items():
                need[e].add(s)
            if me[0] != "dma":
                last[me[0]] = me[1]
        for e, s in last.items():
            need[e].add(s)
        cnt = {}
        for e in CENG:
            c = self.sig[e]
            m = {}
            for s in sorted(need[e]):
                c += 1
                m[s] = c
            cnt[e] = m
            self.sig[e] = c
        for eng, fn, deps_eng, deps_dma, me in self.ops:
            E = self.eng[eng]
            for e, s in deps_eng.items():
                c = cnt[e][s]
                if self.seen_eng[eng].get(e, 0) >= c:
                    continue
                E.wait_ge(self.sems[e], c)
                self.nwait += 1
                self.seen_eng[eng][e] = c
            for d in sorted(deps_dma):
                if d < self.dma_done_upto or d in self.seen_dma[eng]:
                    continue
                E.wait_ge(self.dsems[d % NDMA], 16 * (d // NDMA + 1))
                self.nwait += 1
                self.seen_dma[eng].add(d)
            ins = fn()
            self.nops += 1
            if me[0] == "dma":
                ins.then_inc(self.dsems[me[1] % NDMA], 16)
            elif me[1] in cnt[me[0]]:
                ins.then_inc(self.sems[me[0]], 1)
        self.ops = []
        self.nbar += 1
        sp = self.eng["sp"]
        for dd in range(max(self.dma_done_upto, self.ndma - NDMA), self.ndma):
            if dd in self.seen_dma["sp"]:
                continue
            sp.wait_ge(self.dsems[dd % NDMA], 16 * (dd // NDMA + 1))
        sp.sem_inc(self.bar, 1)
        for en, E in self.eng.items():
            for e2 in CENG:
                if self.seen_eng[en].get(e2, 0) < self.sig[e2]:
                    E.wait_ge(self.sems[e2], self.sig[e2])
                    self.seen_eng[en][e2] = self.sig[e2]
            E.wait_ge(self.bar, self.nbar)
        self.dma_done_upto = self.ndma
        for e in self.eng:
            self.seen_dma[e] = set()
        for r in Res.ALL:
            r.clear()

    def finish(self):
        self.flush(final=True)
        while self.scopes:
            self.scopes.pop().close()
        self.es.close()

    def mm(self, out, lhsT, rhs, start=True, stop=True):
        nc = self.nc
        self._rec("pe", lambda: nc.tensor.matmul(out.ap, lhsT.ap, rhs.ap, start=start, stop=stop),
                  [lhsT, rhs], [out])

    def tr(self, out, in_, ident):
        nc = self.nc
        self._rec("pe", lambda: nc.tensor.transpose(out.ap, in_.ap, ident.ap), [in_, ident], [out])

    def act(self, out, in_, func, bias=None, scale=1.0):
        nc = self.nc
        kw = {}
        if bias is not None:
            kw["bias"] = bias.ap if isinstance(bias, V) else bias
        sc = scale.ap if isinstance(scale, V) else scale
        self._rec("act", lambda: nc.scalar.activation(out.ap, in_.ap, func, scale=sc, **kw),
                  [in_, bias, scale], [out])

    def tt(self, eng, out, in0, in1, op):
        E = self.eng[eng]
        self._rec(eng, lambda: E.tensor_tensor(out.ap, in0.ap, in1.ap, op), [in0, in1], [out])

    def ts(self, eng, out, in0, s1, s2=None, op0=ALU.mult, op1=None):
        E = self.eng[eng]
        a1 = s1.ap if isinstance(s1, V) else s1
        a2 = s2.ap if isinstance(s2, V) else s2
        kw = {}
        if op1 is not None:
            kw["op1"] = op1
        self._rec(eng, lambda: E.tensor_scalar(out.ap, in0.ap, a1, a2, op0, **kw), [in0, s1, s2], [out])

    def stt(self, eng, out, in0, scalar, in1, op0, op1):
        E = self.eng[eng]
        a = scalar.ap if isinstance(scalar, V) else scalar
        self._rec(eng, lambda: E.scalar_tensor_tensor(out.ap, in0.ap, a, in1.ap, op0, op1),
                  [in0, scalar, in1], [out])

    def copy(self, eng, out, in_):
        if eng == "act":
            nc = self.nc
            self._rec("act", lambda: nc.scalar.copy(out.ap, in_.ap), [in_], [out])
        else:
            E = self.eng[eng]
            self._rec(eng, lambda: E.tensor_copy(out.ap, in_.ap), [in_], [out])

    def memset(self, eng, out, val):
        E = self.eng[eng]
        self._rec(eng, lambda: E.memset(out.ap, val), [], [out])

    def recip(self, out, in_):
        nc = self.nc
        self._rec("dve", lambda: nc.vector.reciprocal(out.ap, in_.ap), [in_], [out])

    def dma(self, q, out, in_):
        E = self.eng[q]
        self._rec(q, lambda: E.dma_start(out=out.ap, in_=in_.ap), [in_], [out], dma=True)


def _perm(n2):
    n = n2 // 2
    return np.array([j + n if j < n else j - n for j in range(n2)])


def _axial_perm(d):
    h = d // 2
    p = _perm(h)
    return np.concatenate([p, p + h])


def _rope_tables(d, nlat, grid_w):
    h = d // 2
    n = h // 2
    t = np.arange(nlat)
    row = (t // grid_w).astype(np.float32)
    col = (t % grid_w).astype(np.float32)
    inv = np.power(np.float32(10000.0), -np.arange(n, dtype=np.float32) / np.float32(n)).astype(np.float32)
    C = np.zeros((d, nlat), np.float32)
    Sg = np.zeros((d, nlat), np.float32)
    for half, pos in ((0, row), (1, col)):
        ang = (pos[None, :] * inv[:, None]).astype(np.float32)
        c, s = np.cos(ang), np.sin(ang)
        b = half * h
        C[b:b + n] = c
        C[b + n:b + h] = c
        Sg[b:b + n] = -s
        Sg[b + n:b + h] = s
    return C, Sg


def fm(v, nchunk):
    sh = v.shape[:-1]
    return np.ascontiguousarray(np.swapaxes(v.reshape(sh + (nchunk, 128)), -1, -2))


def prep_inputs(inp, nlat, depth, ncores, grid_w=64):
    T = CTX + nlat
    L = depth
    f32 = np.float32
    w_in = np.asarray(inp["w_in"], f32)[:L]
    o_qa, o_kva, o_kr, o_z, o_xbc, o_dt, o_swq, o_swk, o_swv = np.cumsum([0, 256, 128, 32, 384, 896, 12, 256, 128])
    p32, p64 = _axial_perm(32), _axial_perm(64)
    cols = []
    cols += list(range(o_qa, o_qa + 256))
    cols += list(range(o_kva, o_kva + 128))
    cols += list(range(o_kr, o_kr + 32))
    cols += list(o_kr + p32)
    cols += list(range(o_z, o_z + 384))
    cols += list(range(o_xbc, o_xbc + 896))
    cols += list(range(o_swq, o_swq + 256))
    for h in range(4):
        cols += list(o_swq + h * 64 + p64)
    for g in range(2):
        kk = list(range(o_swk + g * 64, o_swk + g * 64 + 64))
        cols += kk + kk
    for g in range(2):
        kk = list(o_swk + g * 64 + p64)
        cols += kk + kk
    cols += list(range(o_swv, o_swv + 128))
    cols += list(range(o_dt, o_dt + 12))
    cols = np.array(cols)
    assert len(cols) == NA
    w_inA = np.ascontiguousarray(w_in[:, :, cols])
    w_uq = np.asarray(inp["mla_w_uq"], f32)[:L]
    pq = np.arange(576)
    for h in range(6):
        pq[h * 96 + 64:h * 96 + 96] = h * 96 + 64 + p32
    w_uq2 = np.ascontiguousarray(np.stack([w_uq, w_uq[:, :, pq]], axis=2))
    w_ukv = np.asarray(inp["mla_w_ukv"], f32)[:L].reshape(L, 128, 6, 128)
    w_ukvk = np.ascontiguousarray(w_ukv[:, :, :, :64].reshape(L, 128, 384))
    w_ukvv = np.ascontiguousarray(w_ukv[:, :, :, 64:].reshape(L, 128, 384))
    cq = np.ones((96, T), f32)
    sq = np.zeros((96, T), f32)
    c32, s32 = _rope_tables(32, nlat, grid_w)
    cq[64:, CTX:] = c32
    sq[64:, CTX:] = s32
    c64, s64 = _rope_tables(64, nlat, grid_w)
    cs = np.ones((128, T), f32)
    ss = np.zeros((128, T), f32)
    cs[:64, CTX:] = c64
    cs[64:, CTX:] = c64
    ss[:64, CTX:] = s64
    ss[64:, CTX:] = s64
    jj = np.arange(128)
    umat = (jj[:, None] <= jj[None, :]).astype(f32)
    lmat = (jj[:, None] >= jj[None, :]).astype(f32)

    def rep(v):
        v = np.asarray(v, f32)[:L].reshape(L, 1, -1)
        return np.ascontiguousarray(np.broadcast_to(v, (L, 128, v.shape[-1])))

    ssm_d = np.asarray(inp["ssm_d"], f32)[:L]
    dcol = np.zeros((L, 128, 3), f32)
    for c in range(3):
        dcol[:, :64, c] = ssm_d[:, 2 * c:2 * c + 1]
        dcol[:, 64:, c] = ssm_d[:, 2 * c + 1:2 * c + 2]
    shared = {
        "w_mod": np.ascontiguousarray(np.asarray(inp["w_mod"], f32)[:L]),
        "b_modT": fm(np.asarray(inp["b_mod"], f32)[:L], 48),
        "n1g": fm(np.asarray(inp["norm1_g"], f32)[:L], 8),
        "n2g": fm(np.asarray(inp["norm2_g"], f32)[:L], 8),
        "fng": fm(np.asarray(inp["final_norm_g"], f32), 8),
        "w_inA": w_inA,
        "gq": fm(np.asarray(inp["mla_q_norm_g"], f32)[:L], 2),
        "gkv": fm(np.asarray(inp["mla_kv_norm_g"], f32)[:L], 1),
        "w_uq2": w_uq2,
        "w_ukvk": w_ukvk,
        "w_ukvv": w_ukvv,
        "cw": np.ascontiguousarray(np.transpose(
            np.asarray(inp["ssm_conv_w"], f32)[:L].reshape(L, 3, 7, 128), (0, 3, 2, 1))),
        "cb": fm(np.asarray(inp["ssm_conv_b"], f32)[:L], 7),
        "dtb": rep(np.asarray(inp["ssm_dt_bias"], f32)[:L].reshape(L, 12)),
        "alog": rep(np.asarray(inp["ssm_a_log"], f32)[:L].reshape(L, 12)),
        "dcol": dcol,
        "sng": fm(np.asarray(inp["ssm_norm_g"], f32)[:L], 3),
        "sink": rep(inp["swa_sink"]),
        "w_out": np.ascontiguousarray(np.asarray(inp["w_out"], f32)[:L]),
        "w_up": np.ascontiguousarray(np.asarray(inp["ffn_w_up"], f32)[:L]),
        "fcw": np.ascontiguousarray(np.transpose(
            np.asarray(inp["ffn_conv_w"], f32)[:L].reshape(L, 3, FC, 128), (0, 3, 2, 1))),
        "fcb": fm(np.asarray(inp["ffn_conv_b"], f32)[:L], FC),
        "w_dn": np.ascontiguousarray(np.asarray(inp["ffn_w_down"], f32)[:L]),
        "cq": cq, "sq": sq, "cs": cs, "ss": ss,
        "ident": np.eye(128, dtype=f32), "umat": umat, "lmat": lmat,
    }
    x = np.asarray(inp["x"], f32)
    c = np.asarray(inp["c"], f32)
    ctx = np.asarray(inp["ctx"], f32)
    c_ctx = np.asarray(inp["c_ctx"], f32)
    maps = []
    for b in range(ncores):
        cv = np.stack([fm(c[b], 8), fm(c_ctx, 8)], axis=-1)
        m = dict(shared)
        m["x_in"] = np.ascontiguousarray(x[b, :nlat])
        m["ctx_in"] = np.ascontiguousarray(ctx[b])
        m["cvec"] = np.ascontiguousarray(cv)
        maps.append(m)
    return maps


def build_program(nlat, depth, debug_outs=(), stop_after=None):
    L = depth
    T = CTX + nlat
    NB = T // 128
    groups = [(0, CTX, 1)] + [(CTX + 512 * j, 512, 0) for j in range(nlat // 512)]
    NG = len(groups)
    Res.ALL = []
    nc = bass.Bass("TRN2", target_bir_lowering=False)
    S = Sched(nc)
    RO = Res("readonly")

    def din(name, shape):
        return V(nc.dram_tensor(name, list(shape), F32, kind="ExternalInput").ap(), RO)

    x_in = din("x_in", [nlat, D])
    ctx_in = din("ctx_in", [CTX, D])
    cvec = din("cvec", [128, 8, 2])
    w_mod = din("w_mod", [L, D, 6 * D])
    b_modT = din("b_modT", [L, 128, 48])
    n1g_d = din("n1g", [L, 128, 8])
    n2g_d = din("n2g", [L, 128, 8])
    fng_d = din("fng", [128, 8])
    w_inA = din("w_inA", [L, D, NA])
    gq_d = din("gq", [L, 128, 2])
    gkv_d = din("gkv", [L, 128, 1])
    w_uq2 = din("w_uq2", [L, 256, 2, 576])
    w_ukvk = din("w_ukvk", [L, 128, 384])
    w_ukvv = din("w_ukvv", [L, 128, 384])
    cw_d = din("cw", [L, 128, 7, 3])
    cb_d = din("cb", [L, 128, 7])
    dtb_d = din("dtb", [L, 128, 12])
    alog_d = din("alog", [L, 128, 12])
    dcol_d = din("dcol", [L, 128, 3])
    sng_d = din("sng", [L, 128, 3])
    sink_d = din("sink", [L, 128, 4])
    w_out = din("w_out", [L, D, D])
    w_up = din("w_up", [L, D, 2 * FFN])
    fcw_d = din("fcw", [L, 128, FC, 3])
    fcb_d = din("fcb", [L, 128, FC])
    w_dn = din("w_dn", [L, FFN, D])
    cq_d = din("cq", [96, T])
    sq_d = din("sq", [96, T])
    cs_d = din("cs", [128, T])
    ss_d = din("ss", [128, T])
    ident_d = din("ident", [128, 128])
    umat_d = din("umat", [128, 128])
    lmat_d = din("lmat", [128, 128])
    out_d = nc.dram_tensor("out", [nlat, D], F32, kind="ExternalOutput").ap()

    dbg = {}

    def scratch(name, shape, dt):
        kind = "ExternalOutput" if name in debug_outs else "Internal"
        return nc.dram_tensor(name, list(shape), dt, kind=kind).ap()

    XT = scratch("XT", [8, 128, T], F32)
    QT = scratch("QT", [6, 96, T], BF16)
    KT = scratch("KT", [6, 96, T], BF16)
    VA = scratch("VA", [NB, 128, 390], BF16)
    QS = scratch("QS", [2, 128, T], BF16)
    KS = scratch("KS", [2, 128, T], BF16)
    VS = scratch("VS", [NB, 128, 130], BF16)
    ZT = scratch("ZT", [3, 128, T], BF16)
    XBC = scratch("XBC", [7, 128, T], F32)
    DTD = scratch("DTD", [NB, 128, 12], F32)
    YT = scratch("YT", [8, 128, T], BF16)
    YF = scratch("YF", [3, 128, T], F32)
    H2 = scratch("H2", [8, 128, T], BF16)
    GV = scratch("GV", [2 * FC, 128, T], BF16)

    def dv(ap, *key):
        return V(ap, S.R(*key))

    ident = S.sb("ident", [128, 128], F32)
    umat = S.sb("umat", [128, 128], F32)
    lmat = S.sb("lmat", [128, 128], F32)
    identb = S.sb("identb", [128, 128], BF16)
    umatb = S.sb("umatb", [128, 128], BF16)
    lmatb = S.sb("lmatb", [128, 128], BF16)
    onesf = S.sb("onesf", [128, 128], F32)
    onesb = S.sb("onesb", [128, 128], BF16)
    scv = S.sb("scv", [128, 8, 2], F32)
    epsb = S.sb("epsb", [128, 1], F32)
    modT = S.sb("modT", [128, 48, 2], F32)
    gsc = S.sb("gsc", [128, 2, 8, 2], F32)
    PS = [S.ps("psb%d" % i, [128, 512], F32) for i in range(7)]
    PSB = S.ps("psbf", [128, 1024], BF16)
    S.dma("sp", ident, ident_d)
    S.dma("sp", umat, umat_d)
    S.dma("sp", lmat, lmat_d)
    S.dma("sp", scv, cvec)
    S.copy("dve", identb, ident)
    S.copy("dve", umatb, umat)
    S.copy("dve", lmatb, lmat)
    S.memset("pool", onesf, 1.0)
    S.memset("pool", onesb, 1.0)
    S.memset("pool", epsb, EPS)
    S.act(scv, scv, AF.Silu)
    S.flush()

    wstg = []
    wcnt = [0]

    def alloc_wstg():
        wstg[:] = [S.sb("wstg%d" % i, [128, 2048], F32) for i in range(3)]

    def load_w1(dst, src, w):
        st = wstg[wcnt[0] % 3]
        eng = "act" if wcnt[0] % 2 == 0 else "dve"
        wcnt[0] += 1
        S.dma("sp", st[:, :w], src)
        S.copy(eng, dst, st[:, :w])

    def load_w(dst, src, ncols):
        for c0 in range(0, ncols, 2048):
            w = min(2048, ncols - c0)
            load_w1(dst[:, c0:c0 + w], src[:, c0:c0 + w], w)

    def load_w_kc(dst, src_l, ncols):
        for c0 in range(0, ncols, 2048):
            w = min(2048, ncols - c0)
            for kc in range(8):
                load_w1(dst[:, kc, c0:c0 + w], src_l[kc * 128:(kc + 1) * 128, c0:c0 + w], w)

    psi = [0]
    nrot = [6]

    def nps():
        psi[0] = (psi[0] + 1) % nrot[0]
        return PS[psi[0]]

    def rms_rstd(chunks, n, dim, sqt, rstd):
        ps = nps()
        for i, ch in enumerate(chunks):
            S.act(sqt[:, i, :n], ch, AF.Square)
            S.mm(ps[:, :n], onesb, sqt[:, i, :n], start=(i == 0), stop=(i == len(chunks) - 1))
        S.act(rstd[:, :n], ps[:, :n], AF.Ln, bias=epsb[:, 0:1], scale=1.0 / dim)
        S.act(rstd[:, :n], rstd[:, :n], AF.Exp, scale=-0.5)

    def xt_view(g):
        t0, n, j = groups[g]
        return dv(XT[:, :, t0:t0 + n].rearrange("k p t -> p k t"), "XT", g)

    S.push()
    xin_t = [S.sb("xin%d" % i, [128, D], F32) for i in range(2)]
    xst = [S.sb("xst%d" % i, [128, 8, 512], F32) for i in range(2)]
    bi = 0
    for g, (t0, n, j) in enumerate(groups):
        st = xst[g % 2]
        for tb in range(n // 128):
            xi = xin_t[bi % 2]
            bi += 1
            src = ctx_in[tb * 128:(tb + 1) * 128, :] if j == 1 else \
                x_in[t0 - CTX + tb * 128:t0 - CTX + (tb + 1) * 128, :]
            S.dma("sp", xi, src)
            for half in range(2):
                ps = nps()
                for k4 in range(4):
                    kc = half * 4 + k4
                    S.tr(ps[:, k4 * 128:(k4 + 1) * 128], xi[:, kc * 128:(kc + 1) * 128], ident)
                S.copy("dve" if half == 0 else "act", st[:, half * 4:half * 4 + 4, tb * 128:(tb + 1) * 128],
                       ps.re("p (k t) -> p k t", k=4))
        S.dma("sp", xt_view(g), st[:, :, :n])
    S.pop("S0")

    if stop_after == "S0":
        S.finish()
        return nc, S
    for l in range(L):
        last = (l == L - 1)
        S.push()
        wm = [S.sb("wm%d" % i, [128, 8, 512], F32) for i in range(4)]
        bm = S.sb("bm", [128, 48], F32)
        n1g = S.sb("n1g", [128, 8], F32)
        n2g = S.sb("n2g", [128, 8], F32)
        S.dma("sp", bm, b_modT[l])
        S.dma("sp", n1g, n1g_d[l])
        S.dma("sp", n2g, n2g_d[l])
        for nb in range(12):
            w = wm[nb % 4]
            S.dma("sp", w, w_mod[l, :, nb * 512:(nb + 1) * 512].re("(k p) n -> p k n", p=128))
            for jj in range(4):
                oc = nb * 4 + jj
                ps = nps()
                for kc in range(8):
                    S.mm(ps[:, 0:2], w[:, kc, jj * 128:(jj + 1) * 128], scv[:, kc, :], start=(kc == 0), stop=(kc == 7))
                S.ts("dve", modT[:, oc, :], ps[:, 0:2], bm[:, oc:oc + 1], None, op0=ALU.add)
        for jx in range(2):
            S.stt("dve", gsc[:, 0, :, jx], modT[:, 8:16, jx], 1.0, n1g, ALU.add, ALU.mult)
            S.stt("dve", gsc[:, 1, :, jx], modT[:, 32:40, jx], 1.0, n2g, ALU.add, ALU.mult)
        S.pop("S1")

        if stop_after == "S1":
            break
        S.push()
        alloc_wstg()
        Win = S.sb("Win", [128, 8, NA], BF16)
        load_w_kc(Win, w_inA[l], NA)
        Wuq = S.sb("Wuq", [128, 2, 2, 576], BF16)
        for c in range(2):
            load_w(Wuq[:, c, :, :].re("p v n -> p (v n)"), w_uq2[l, c * 128:(c + 1) * 128].re("p v n -> p (v n)"), 1152)
        Wkk = S.sb("Wkk", [128, 384], BF16)
        Wkv = S.sb("Wkv", [128, 384], BF16)
        load_w(Wkk, w_ukvk[l], 384)
        load_w(Wkv, w_ukvv[l], 384)
        gq = S.sb("gq", [128, 2], F32)
        gkv = S.sb("gkv", [128, 1], F32)
        dtb = S.sb("dtb", [128, 12], F32)
        S.dma("sp", gq, gq_d[l])
        S.dma("sp", gkv, gkv_d[l])
        S.dma("sp", dtb, dtb_d[l])
        xT2 = [S.sb("xT%d" % i, [128, 8, 512], F32) for i in range(2)]
        sqt = S.sb("sqt", [128, 8, 512], BF16)
        rstd = S.sb("rstd", [128, 512], F32)
        tmpf = [S.sb("tmpf%d" % i, [128, 512], F32) for i in range(4)]
        rp = [0]

        def rpair():
            rp[0] += 1
            return (tmpf[0], tmpf[1]) if rp[0] % 2 == 0 else (tmpf[2], tmpf[3])
        hT = S.sb("hT", [128, 8, 512], BF16)
        qaT = S.sb("qaT", [128, 2, 512], F32)
        kvaT = S.sb("kvaT", [128, 512], F32)
        krT = S.sb("krT", [32, 512], F32)
        krpT = S.sb("krpT", [32, 512], F32)
        ob16 = [S.sb("ob16_%d" % i, [128, 512], BF16) for i in range(3)]
        of32 = [S.sb("of32_%d" % i, [128, 512], F32) for i in range(2)]
        cqt = S.sb("cqt", [96, 512], F32)
        sqq = S.sb("sqq", [96, 512], F32)
        cst = S.sb("cst", [128, 512], F32)
        sst = S.sb("sst", [128, 512], F32)
        ckt = S.sb("ckt", [32, 512], F32)
        sqtk = S.sb("sqtk", [128, 1, 512], BF16)
        rstdk = S.sb("rstdk", [128, 512], F32)
        ktm = [tmpf[0][:32], tmpf[1][:32]]
        skt = S.sb("skt", [32, 512], F32)
        qan = S.sb("qan", [128, 2, 512], BF16)
        kvan = S.sb("kvan", [128, 512], BF16)
        krot = S.sb("krot", [32, 512], BF16)
        vs_t = [S.sb("vs_t%d" % i, [128, 2, 65], BF16) for i in range(2)]
        va_t = [S.sb("va_t%d" % i, [128, 6, 65], BF16) for i in range(2)]
        dt_t = [S.sb("dt_t%d" % i, [128, 4, 12], F32) for i in range(4)]
        for t_ in vs_t + va_t:
            S.memset("pool", t_, 1.0)
        cnt16 = [0]
        cnt32 = [0]

        def o16():
            cnt16[0] += 1
            return ob16[cnt16[0] % 3]

        def o32():
            cnt32[0] += 1
            return of32[cnt32[0] % 2]

        hT2 = [hT, S.sb("hTb", [128, 8, 512], BF16)]
        sqtN = [S.sb("sqtN", [128, 8, 512], BF16)] * 2
        rstdN = [S.sb("rstdN%d" % i, [128, 512], F32) for i in range(2)]
        ntf = [S.sb("ntf%d" % i, [128, 512], F32) for i in range(2)]

        def norm1(g_):
            t0_, n_, j_ = groups[g_]
            xT_ = xT2[g_ % 2]
            rms_rstd([xT_[:, kc, :n_] for kc in range(8)], n_, D, sqtN[g_ % 2], rstdN[g_ % 2])
            for kc in range(8):
                tf = ntf[kc % 2]
                S.stt("dve", tf[:, :n_], xT_[:, kc, :n_], gsc[:, 0, kc, j_:j_ + 1], rstdN[g_ % 2][:, :n_], ALU.mult, ALU.mult)
                S.act(hT2[g_ % 2][:, kc, :n_], tf[:, :n_], AF.Identity, bias=modT[:, kc, j_:j_ + 1])

        S.dma("sp", xT2[0][:, :, :groups[0][1]], xt_view(0))
        if NG > 1:
            S.dma("sp", xT2[1][:, :, :groups[1][1]], xt_view(1))
        norm1(0)
        for g, (t0, n, j) in enumerate(groups):
            hT = hT2[g % 2]
            if g + 1 < NG:
                norm1(g + 1)
            if g + 2 < NG:
                S.dma("sp", xT2[g % 2][:, :, :groups[g + 2][1]], xt_view(g + 2))
            S.dma("sp", cqt[:, :n], cq_d[:, t0:t0 + n])
            S.dma("sp", sqq[:, :n], sq_d[:, t0:t0 + n])
            S.dma("sp", cst[:, :n], cs_d[:, t0:t0 + n])
            S.dma("sp", sst[:, :n], ss_d[:, t0:t0 + n])
            S.dma("sp", ckt[:, :n], cq_d[64:96, t0:t0 + n])
            S.dma("sp", skt[:, :n], sq_d[64:96, t0:t0 + n])

            def fmchunk(c0, m):
                ps = nps()
                for kc in range(8):
                    S.mm(ps[:m, :n], Win[:, kc, c0:c0 + m], hT[:, kc, :n], start=(kc == 0), stop=(kc == 7))
                return ps

            for c in range(2):
                S.copy("act", qaT[:, c, :n], fmchunk(c * 128, 128)[:, :n])
            S.copy("act", kvaT[:, :n], fmchunk(256, 128)[:, :n])
            S.copy("dve", krT[:, :n], fmchunk(384, 32)[:32, :n])
            S.copy("dve", krpT[:, :n], fmchunk(416, 32)[:32, :n])
            for c in range(3):
                ob = o16()
                S.act(ob[:, :n], fmchunk(448 + c * 128, 128)[:, :n], AF.Silu)
                S.dma("sp", dv(ZT[c, :, t0:t0 + n], "ZT", g, c), ob[:, :n])
            rms_rstd([qaT[:, c, :n] for c in range(2)], n, 256, sqt, rstd)
            for c in range(2):
                S.stt("dve", qan[:, c, :n], qaT[:, c, :n], gq[:, c:c + 1], rstd[:, :n], ALU.mult, ALU.mult)
            rms_rstd([kvaT[:, :n]], n, 128, sqtk, rstdk)
            S.stt("dve", kvan[:, :n], kvaT[:, :n], gkv[:, 0:1], rstdk[:, :n], ALU.mult, ALU.mult)
            S.tt("dve", ktm[0][:, :n], krT[:, :n], ckt[:, :n], ALU.mult)
            S.tt("dve", ktm[1][:, :n], krpT[:, :n], skt[:, :n], ALU.mult)
            S.tt("pool", krot[:, :n], ktm[0][:, :n], ktm[1][:, :n], ALU.add)
            for h in range(6):
                S.dma("sp", dv(KT[h, 64:96, t0:t0 + n], "KTr", g, h), krot[:, :n])
            for c in range(7):
                of = o32()
                S.copy("dve" if c % 2 == 0 else "act", of[:, :n], fmchunk(832 + c * 128, 128)[:, :n])
                S.dma("sp", dv(XBC[c, :, t0:t0 + n], "XBC", g, c), of[:, :n])
            for kind, base, dst in (("q", 1728, QS), ("k", 2240, KS)):
                for c in range(2):
                    pa = fmchunk(base + c * 128, 128)
                    pb = fmchunk(base + 256 + c * 128, 128)
                    ta, tb_ = rpair()
                    S.tt("dve", ta[:, :n], pa[:, :n], cst[:, :n], ALU.mult)
                    S.tt("dve", tb_[:, :n], pb[:, :n], sst[:, :n], ALU.mult)
                    ob = o16()
                    S.tt("pool", ob[:, :n], ta[:, :n], tb_[:, :n], ALU.add)
                    S.dma("sp", dv(dst[c, :, t0:t0 + n], "S" + kind, g, c), ob[:, :n])
            for tb in range(n // 128):
                blk = t0 // 128 + tb
                ps = nps()
                for kc in range(8):
                    S.mm(ps[:, 0:140], hT[:, kc, tb * 128:(tb + 1) * 128], Win[:, kc, 2752:2892],
                         start=(kc == 0), stop=(kc == 7))
                vt = vs_t[blk % 2]
                S.copy("act", vt[:, :, 0:64], ps[:, 0:128].re("p (g d) -> p g d", g=2))
                S.dma("sp", dv(VS[blk], "VS", blk), vt.re("p g d -> p (g d)"))
                dtt = dt_t[blk % 4]
                S.tt("dve", dtt[:, 0, :], ps[:, 128:140], dtb, ALU.add)
                S.stt("dve", dtt[:, 1, :], dtt[:, 0, :], -1.0, dtt[:, 0, :], ALU.mult, ALU.max)
                S.act(dtt[:, 2, :], dtt[:, 1, :], AF.Exp, scale=-1.0)
                S.act(dtt[:, 2, :], dtt[:, 2, :], AF.Ln, bias=1.0)
                S.stt("dve", dtt[:, 3, :], dtt[:, 0, :], 0.0, dtt[:, 2, :], ALU.max, ALU.add)
                S.dma("sp", dv(DTD[blk], "DTD", blk), dtt[:, 3, :])
            for h in range(6):
                pa = nps()
                pb = nps()
                for c in range(2):
                    S.mm(pa[:96, :n], Wuq[:, c, 0, h * 96:(h + 1) * 96], qan[:, c, :n], start=(c == 0), stop=(c == 1))
                for c in range(2):
                    S.mm(pb[:96, :n], Wuq[:, c, 1, h * 96:(h + 1) * 96], qan[:, c, :n], start=(c == 0), stop=(c == 1))
                ta, tb_ = rpair()
                S.tt("dve", ta[:96, :n], pa[:96, :n], cqt[:, :n], ALU.mult)
                S.tt("dve", tb_[:96, :n], pb[:96, :n], sqq[:, :n], ALU.mult)
                ob = o16()
                S.tt("pool", ob[:96, :n], ta[:96, :n], tb_[:96, :n], ALU.add)
                S.dma("sp", dv(QT[h, :, t0:t0 + n], "QT", g, h), ob[:96, :n])
            for hp in range(3):
                ps = nps()
                S.mm(ps[:, :n], Wkk[:, hp * 128:(hp + 1) * 128], kvan[:, :n])
                ob = o16()
                S.copy("act", ob[:, :n], ps[:, :n])
                for e in range(2):
                    S.dma("sp", dv(KT[2 * hp + e, 0:64, t0:t0 + n], "KTn", g, 2 * hp + e),
                          ob[e * 64:(e + 1) * 64, :n])
            for tb in range(n // 128):
                blk = t0 // 128 + tb
                ps = nps()
                S.mm(ps[:, 0:384], kvan[:, tb * 128:(tb + 1) * 128], Wkv)
                vt = va_t[blk % 2]
                S.copy("dve", vt[:, :, 0:64], ps[:, 0:384].re("p (h d) -> p h d", h=6))
                S.dma("sp", dv(VA[blk], "VA", blk), vt.re("p h d -> p (h d)"))
        S.pop("S2")

        if stop_after == "S2":
            break
        S.push()
        KTs = S.sb("KTs", [96, 6, T], BF16)
        VAs = S.sb("VAs", [128, NB, 390], BF16)
        kres = [S.R("KTr", g, h) for g in range(NG) for h in range(6)] + \
               [S.R("KTn", g, h) for g in range(NG) for h in range(6)]
        vres = [S.R("VA", b) for b in range(NB)]
        nv = 4
        vb = [(i * NB) // nv for i in range(nv + 1)]
        S.dma("sp", KTs[:, 0, :], V(KT[0], kres))
        for i in range(nv):
            S.dma("sp", VAs[:, vb[i]:vb[i + 1], :], V(VA[vb[i]:vb[i + 1]].rearrange("b p f -> p b f"), vres))
        for h in range(1, 6):
            S.dma("sp", KTs[:, h, :], V(KT[h], kres))
        Qg = [S.sb("Qg%d" % i, [96, 6, 512], BF16) for i in range(2)]
        Pt = [S.sb("Pt%d" % i, [128, 512], BF16) for i in range(4)]
        Rr = S.sb("Rr", [128, 512], F32)
        rb = S.sb("rb", [64, 512], F32)
        yo = [S.sb("yo%d" % i, [64, 512], BF16) for i in range(2)]
        sc_mla = float(96 ** -0.5)
        LAG = 2
        qgroups = [g for g in range(NG) if not (last and groups[g][2] == 1)]

        def loadq(g):
            t0, n, j = groups[g]
            S.dma("sp", Qg[g % 2][:, :, :n],
                  V(QT[:, :, t0:t0 + n].rearrange("h p t -> p h t"), [S.R("QT", g, h) for h in range(6)]))

        loadq(qgroups[0])
        hcount = 0
        pend = []
        for qi, g in enumerate(qgroups):
            t0, n, j = groups[g]
            if qi + 1 < len(qgroups):
                loadq(qgroups[qi + 1])
            Q = Qg[g % 2]
            nkb = 2 if j == 1 else NB
            for h in range(6):
                O = PS[4 + hcount % 2]
                for i in range(nkb + LAG):
                    if i < nkb:
                        S.mm(PS[i % 4][:, :n], KTs[:, h, i * 128:(i + 1) * 128], Q[:, h, :n])
                        S.act(Pt[i % 4][:, :n], PS[i % 4][:, :n], AF.Exp, scale=sc_mla)
                    if i >= LAG:
                        kb = i - LAG
                        S.mm(O[:65, :n], VAs[:, kb, h * 65:(h + 1) * 65], Pt[kb % 4][:, :n],
                             start=(kb == 0), stop=(kb == nkb - 1))
                    if i == (10 if nkb > 12 else 3) and pend:
                        pend.pop(0)()

                def epi(O=O, n=n, t0=t0, g=g, h=h, y=yo[hcount % 2]):
                    S.recip(Rr[64:65, :n], O[64:65, :n])
                    S.mm(PS[6][:64, :n], onesf[64:65, 0:64], Rr[64:65, :n])
                    S.copy("dve", rb[:, :n], PS[6][:64, :n])
                    S.tt("dve", y[:, :n], O[0:64, :n], rb[:, :n], ALU.mult)
                    S.dma("sp", dv(YT[h // 2, (h % 2) * 64:(h % 2) * 64 + 64, t0:t0 + n], "YTa", g, h), y[:, :n])

                pend.append(epi)
                hcount += 1
        while pend:
            pend.pop(0)()
        S.pop("S3")

        if stop_after == "S3":
            break
        S.push()
        QSs = S.sb("QSs", [128, 2, T], BF16)
        KSs = S.sb("KSs", [128, 2, T], BF16)
        VSs = S.sb("VSs", [128, NB, 130], BF16)
        S.dma("sp", QSs, V(QS.rearrange("c p t -> p c t"), [S.R("Sq", g, c) for g in range(NG) for c in range(2)]))
        S.dma("sp", KSs, V(KS.rearrange("c p t -> p c t"), [S.R("Sk", g, c) for g in range(NG) for c in range(2)]))
        S.dma("sp", VSs, V(VS.rearrange("b p f -> p b f"), [S.R("VS", b) for b in range(NB)]))
        Qp = [S.sb("Qp%d" % r, [128, 2, T], BF16) for r in range(2)]
        for r in range(2):
            S.memset("pool", Qp[r][(1 - r) * 64:(2 - r) * 64], 0.0)
            S.copy("dve" if r == 0 else "act", Qp[r][r * 64:(r + 1) * 64], QSs[r * 64:(r + 1) * 64])
        es = S.sb("es", [128, 4], F32)
        S.dma("sp", es, sink_d[l])
        S.act(es, es, AF.Exp)
        Pw = [S.sb("Pw%d" % i, [128, 6, 256], BF16) for i in range(2)]
        mlo2 = S.sb("mlo2", [128, 256], BF16)
        mhi2 = S.sb("mhi2", [128, 256], BF16)
        for r in range(2):
            S.copy("dve", mlo2[:, r * 128:(r + 1) * 128], lmatb)
            S.copy("dve", mhi2[:, r * 128:(r + 1) * 128], umatb)
        Rw = S.sb("Rw", [128, 256], F32)
        rbw = S.sb("rbw", [64, 256], F32)
        yw = [S.sb("yw%d" % i, [64, 256], BF16) for i in range(2)]
        it = 0
        pend4 = []
        for qb in range(NB):
            if last and qb < 2:
                continue
            if qb < 2:
                keys = [(0, None), (1, None)]
            else:
                keys = [(0, None), (1, None)]
                if qb - 1 >= 2:
                    keys.append((qb - 1, "lo"))
                keys.append((qb, None))
                if qb + 1 < NB:
                    keys.append((qb + 1, "hi"))
            for gg in range(2):
                P = Pw[it % 2]
                tiles = [PS[(3 * it + q_) % 5] for q_ in range(3)]
                O = PS[5 + it % 2]
                ntile = (len(keys) + 1) // 2
                for idx, (kb, mk) in enumerate(keys):
                    pst = tiles[idx // 2]
                    for r in range(2):
                        S.mm(pst[:, (idx % 2) * 256 + r * 128:(idx % 2) * 256 + (r + 1) * 128],
                             KSs[:, gg, kb * 128:(kb + 1) * 128],
                             Qp[r][:, gg, qb * 128:(qb + 1) * 128])
                for ti in range(ntile):
                    w = min(2, len(keys) - 2 * ti) * 256
                    S.act(P[:, 2 * ti:2 * ti + 2, :].re("p a b -> p (a b)")[:, :w], tiles[ti][:, :w],
                          AF.Exp, scale=0.125)
                for idx, (kb, mk) in enumerate(keys):
                    if mk is not None:
                        m = mlo2 if mk == "lo" else mhi2
                        S.tt("dve", P[:, idx, :], P[:, idx, :], m, ALU.mult)
                for idx, (kb, mk) in enumerate(keys):
                    S.mm(O[:65, :256], VSs[:, kb, gg * 65:(gg + 1) * 65], P[:, idx, :],
                         start=(idx == 0), stop=(idx == len(keys) - 1))

                def epi4(O=O, gg=gg, qb=qb, pbc=tiles[2], y=yw[it % 2]):
                    for r in range(2):
                        S.ts("dve", Rw[64:65, r * 128:(r + 1) * 128], O[64:65, r * 128:(r + 1) * 128],
                             es[64:65, 2 * gg + r:2 * gg + r + 1], None, op0=ALU.add)
                    S.act(Rw[64:65, :], Rw[64:65, :], AF.Ln)
                    S.act(Rw[64:65, :], Rw[64:65, :], AF.Exp, scale=-1.0)
                    S.mm(pbc[:64, 256:512], onesf[64:65, 0:64], Rw[64:65, :])
                    S.copy("dve", rbw, pbc[:64, 256:512])
                    S.tt("dve", y, O[0:64, :256], rbw, ALU.mult)
                    for r in range(2):
                        S.dma("sp", dv(YT[6 + gg, r * 64:(r + 1) * 64, qb * 128:(qb + 1) * 128], "YTc", qb, gg, r),
                              y[:, r * 128:(r + 1) * 128])

                if pend4:
                    pend4.pop(0)()
                pend4.append(epi4)
                it += 1
        while pend4:
            pend4.pop(0)()
        S.pop("S4")

        if stop_after == "S4":
            break
        S.push()
        xbcs = S.sb("xbcs", [128, 7, T], BF16)
        cw = S.sb("cw", [128, 7, 3], F32)
        cb = S.sb("cb", [128, 7], F32)
        S.dma("sp", cw, cw_d[l])
        S.dma("sp", cb, cb_d[l])
        cin = [S.sb("cin%d" % i, [128, 514], F32) for i in range(3)]
        cu = [S.sb("cu%d" % i, [128, 512], F32) for i in range(2)]
        dts = S.sb("dts", [128, NB, 12], F32)
        S.dma("sp", dts, V(DTD.rearrange("b p f -> p b f"), [S.R("DTD", b) for b in range(NB)]))
        Aa = S.sb("Aa", [128, 12], F32)
        S.dma("sp", Aa, alog_d[l])
        S.act(Aa, Aa, AF.Exp)
        S.ts("dve", Aa, Aa, -1.0, None, op0=ALU.mult)
        dta = S.sb("dta", [128, NB, 12], F32)
        S.tt("dve", dta, dts, V(Aa.ap.rearrange("p (o f) -> p o f", o=1).to_broadcast([128, NB, 12]), Aa.res), ALU.mult)
        cum = S.sb("cum", [128, NB, 24], F32)
        for blk in range(NB):
            ps = nps()
            S.mm(ps[:, 0:6], umat, dta[:, blk, 0:6])
            S.mm(ps[:, 6:12], lmat, dta[:, blk, 6:12])
            S.mm(ps[:, 12:24], onesf, dta[:, blk, 0:12])
            S.copy("act", cum[:, blk, :], ps[:, 0:24])
        wdec = S.sb("wdec", [128, NB, 12], F32)
        etot = S.sb("etot", [128, NB, 12], F32)
        dtw = S.sb("dtw", [128, NB, 12], F32)
        S.tt("pool", wdec, cum[:, :, 12:24], cum[:, :, 0:12], ALU.subtract)
        S.act(wdec, wdec, AF.Exp)
        S.act(etot, cum[:, :, 12:24], AF.Exp)
        S.tt("pool", dtw, dts, wdec, ALU.mult)
        xs_tok = S.sb("xs_tok", [128, NB, 384], BF16)
        b_tok = S.sb("b_tok", [128, NB, 256], BF16)
        ci = 0
        for g, (t0, n, j) in enumerate(groups):
            for c in range(7):
                s0, s1 = (0, CTX) if j == 1 else (CTX, T)
                ct = cin[ci % 3]
                u = cu[ci % 2]
                ci += 1
                lo = max(t0 - 1, s0)
                hi = min(t0 + n + 1, s1)
                if lo == t0:
                    S.memset("pool", ct[:, 0:1], 0.0)
                if hi == t0 + n:
                    S.memset("pool", ct[:, n + 1:n + 2], 0.0)
                S.dma("sp", ct[:, lo - (t0 - 1):hi - (t0 - 1)],
                      V(XBC[c, :, lo:hi], [S.R("XBC", gx, c) for gx in range(NG)]))
                S.ts("dve", u[:, :n], ct[:, 0:n], cw[:, c, 0:1], None, op0=ALU.mult)
                S.stt("dve", u[:, :n], ct[:, 1:n + 1], cw[:, c, 1:2], u[:, :n], ALU.mult, ALU.add)
                S.stt("dve", u[:, :n], ct[:, 2:n + 2], cw[:, c, 2:3], u[:, :n], ALU.mult, ALU.add)
                S.act(xbcs[:, c, t0:t0 + n], u[:, :n], AF.Silu, bias=cb[:, c:c + 1])
            for blk in range(t0 // 128, (t0 + n) // 128):
                for c in range(5):
                    S.tr(PSB[:, c * 128:(c + 1) * 128], xbcs[:, c, blk * 128:(blk + 1) * 128], identb)
                S.copy("dve", xs_tok[:, blk, :], PSB[:, 0:384])
                S.copy("act", b_tok[:, blk, :], PSB[:, 384:640])
        dcol = S.sb("dcol", [128, 3], F32)
        sng = S.sb("sng", [128, 3], F32)
        S.dma("sp", dcol, dcol_d[l])
        S.dma("sp", sng, sng_d[l])

        xdtp = [[S.sb("xdtp%d_%d" % (d_, i), [128, 3, 2, 2, 64], BF16) for i in range(2)] for d_ in range(2)]
        xdtw = [[S.sb("xdtw%d_%d" % (d_, i), [128, 6, 64], BF16) for i in range(2)] for d_ in range(2)]
        Hs = [S.sb("Hs%d" % d_, [128, 6, 64], F32) for d_ in range(2)]
        Hbp = [S.sb("Hbp%d" % d_, [128, 3, 2, 2, 64], BF16) for d_ in range(2)]
        cbm = [[S.sb("cbm%d_%d" % (d_, i), [128, 2, 128], BF16) for i in range(2)] for d_ in range(2)]
        NK = 12
        blt = [S.sb("blt%d" % i, [128, 128], F32) for i in range(NK)]
        tdf = [S.sb("tdf%d" % i, [128, 128], F32) for i in range(NK)]
        Mt = [S.sb("Mt%d" % i, [128, 128], BF16) for i in range(NK)]
        e2 = [S.sb("e2_%d" % i, [128, 128], F32) for i in range(NK)]
        Cd = [S.sb("Cd%d" % i, [128, 128], BF16) for i in range(NK)]
        yst = [[S.sb("yst%d_%d" % (d_, i), [128, 3, 128], F32) for i in range(2)] for d_ in range(2)]
        ysum = [S.sb("ysum%d" % i, [128, 3, 512], F32) for i in range(2)]
        yfl = [[S.sb("yfl%d_%d" % (d_, i), [128, 3, 128], F32) for i in range(2)] for d_ in range(2)]
        zs = S.sb("zs", [128, 3, 512], BF16)
        y3 = S.sb("y3", [128, 3, 512], F32)
        sq3 = S.sb("sq3", [128, 3, 512], BF16)
        rstd5 = cu[0]
        yob = S.sb("yob", [128, 3, 512], BF16)
        for d_ in range(2):
            for t_ in xdtp[d_]:
                S.memset("pool", t_, 0.0)
            S.memset("pool", Hs[d_], 0.0)
            S.memset("pool", Hbp[d_], 0.0)
        blk2g = {}
        for g, (t0, n, j) in enumerate(groups):
            for tb in range(n // 128):
                blk2g[t0 // 128 + tb] = g
        kcount = 0
        orders = [list(range(NB)), [1, 0] + list(range(NB - 1, 1, -1))]
        PSY = [PS[6], PS[5]]
        nrot[0] = 5
        seen_blk = set()
        gcount = {}
        kc_ = [0]

        def unit_pro(d, step):
            blk = orders[d][step]
            mask = umatb if d == 0 else lmatb
            tsl = slice(blk * 128, (blk + 1) * 128)
            xp = xdtp[d][step % 2]
            xw = xdtw[d][step % 2]
            xs4 = xs_tok[:, blk, :].re("p (a e f) -> p a e f", a=3, e=2)
            for e in range(2):
                dsel = V(dts.ap[:, blk, d * 6:(d + 1) * 6].rearrange("p (a e o) -> p a e o", a=3, e=2)[:, :, e, :]
                         .to_broadcast([128, 3, 64]), dts.res)
                S.tt("pool", xp[:, :, e, e, :], xs4[:, :, e, :], dsel, ALU.mult)
            S.tt("pool", xw, xs_tok[:, blk, :].re("p (h f) -> p h f", h=6),
                 V(dtw.ap[:, blk, d * 6:(d + 1) * 6].rearrange("p (h o) -> p h o", o=1).to_broadcast([128, 6, 64]),
                   dtw.res), ALU.mult)
            cbt = cbm[d][step % 2]
            for gg in range(2):
                cps = PS[3][:, (2 * d + gg) * 128:(2 * d + gg + 1) * 128]
                S.mm(cps, xbcs[:, 3 + gg, tsl], xbcs[:, 5 + gg, tsl])
                S.tt("dve", cbt[:, gg, :], cps, mask, ALU.mult)

        def PR(k):
            return PS[k // 4][:, (k % 4) * 128:(k % 4 + 1) * 128]

        def ph_A(d, step, hh):
            blk = orders[d][step]
            k = d * 6 + hh
            S.act(blt[k], onesf, AF.Identity, scale=dta[:, blk, k:k + 1])
            S.mm(PR(k), blt[k], umat if d == 0 else lmat)

        def ph_B1(d, step, hh):
            blk = orders[d][step]
            k = d * 6 + hh
            S.ts("dve", tdf[k], PR(k), cum[:, blk, k:k + 1], 0.0, op0=ALU.subtract, op1=ALU.min)

        def ph_B2(d, step, hh):
            k = d * 6 + hh
            S.act(tdf[k], tdf[k], AF.Exp)
            S.act(e2[k], PR(k), AF.Exp)

        def ph_B3(d, step, hh):
            blk = orders[d][step]
            tsl = slice(blk * 128, (blk + 1) * 128)
            k = d * 6 + hh
            gg = hh // 3
            S.tt("dve", Mt[k], tdf[k], cbm[d][step % 2][:, gg, :], ALU.mult)
            S.tt("pool", Cd[k], xbcs[:, 5 + gg, tsl], e2[k], ALU.mult)

        def ph_B4(d, step, hh):
            k = d * 6 + hh
            xp = xdtp[d][step % 2]
            psY = PSY[d]
            pr, e = hh // 2, hh % 2
            S.mm(psY[:, pr * 128:(pr + 1) * 128], xp[:, pr, e, :, :].re("p a f -> p (a f)"), Mt[k],
                 start=(e == 0), stop=False)
            S.mm(psY[:, pr * 128:(pr + 1) * 128], Hbp[d][:, pr, e, :, :].re("p a f -> p (a f)"), Cd[k],
                 start=False, stop=(e == 1))

        def unit_epi(d, step):
            blk = orders[d][step]
            tsl = slice(blk * 128, (blk + 1) * 128)
            xw = xdtw[d][step % 2]
            psY = PSY[d]
            psS = PS[4]
            for gg in range(2):
                S.mm(psS[:, gg * 192:(gg + 1) * 192], b_tok[:, blk, gg * 128:(gg + 1) * 128],
                     xw[:, 3 * gg:3 * gg + 3, :].re("p h f -> p (h f)"))
            S.tt("dve", Hs[d], Hs[d], V(etot.ap[:, blk, d * 6:(d + 1) * 6].rearrange("p (h o) -> p h o", o=1)
                                        .to_broadcast([128, 6, 64]), etot.res), ALU.mult)
            S.tt("dve", Hs[d], Hs[d], psS[:, 0:384].re("p (h f) -> p h f", h=6), ALU.add)
            Hs4 = Hs[d].re("p (a e) f -> p a e f", a=3)
            for e in range(2):
                S.copy("act", Hbp[d][:, :, e, e, :], Hs4[:, :, e, :])
            g = blk2g[blk]
            t0, n, j = groups[g]
            if blk not in seen_blk:
                seen_blk.add(blk)
                ys = yst[d][step % 2]
                S.copy("act", ys, psY[:, 0:384].re("p (a q) -> p a q", a=3))
                S.dma("sp", dv(YF[:, :, tsl].rearrange("c p t -> p c t"), "YF", blk), ys)
            else:
                yf = yfl[d][step % 2]
                S.dma("sp", yf, dv(YF[:, :, tsl].rearrange("c p t -> p c t"), "YF", blk))
                off = blk * 128 - t0
                ysg = ysum[g % 2]
                S.tt("dve", ysg[:, :, off:off + 128], psY[:, 0:384].re("p (a q) -> p a q", a=3), yf, ALU.add)
                gcount[g] = gcount.get(g, 0) + 1
                if gcount[g] == n // 128 and not (last and j == 1):
                    S.dma("sp", zs[:, :, :n], V(ZT[:, :, t0:t0 + n].rearrange("c p t -> p c t"),
                                                [S.R("ZT", g, c) for c in range(3)]))
                    for c in range(3):
                        S.stt("dve", y3[:, c, :n], xbcs[:, c, t0:t0 + n], dcol[:, c:c + 1], ysg[:, c, :n],
                              ALU.mult, ALU.add)
                    S.tt("dve", y3[:, :, :n], y3[:, :, :n], zs[:, :, :n], ALU.mult)
                    rms_rstd([y3[:, c, :n] for c in range(3)], n, 384, sq3, rstd5)
                    for c in range(3):
                        S.stt("dve", yob[:, c, :n], y3[:, c, :n], sng[:, c:c + 1], rstd5[:, :n], ALU.mult, ALU.mult)
                    S.dma("sp", dv(YT[3:6, :, t0:t0 + n].rearrange("c p t -> p c t"), "YTb", g), yob[:, :, :n])

        for step in range(NB):
            unit_pro(0, step)
            unit_pro(1, step)
            for ph in (ph_A, ph_B1, ph_B2, ph_B3):
                for hh in range(6):
                    for d in range(2):
                        ph(d, step, hh)
            for d in range(2):
                for hh in range(6):
                    ph_B4(d, step, hh)
            unit_epi(0, step)
            unit_epi(1, step)
        nrot[0] = 6
        S.pop("S5")

        if stop_after == "S5":
            break
        S.push()
        alloc_wstg()
        Wo = S.sb("Wo", [128, 8, D], BF16)
        for kc in range(8):
            load_w(Wo[:, kc, :], w_out[l, kc * 128:(kc + 1) * 128, :], D)
        yT2 = [S.sb("yT%d" % i, [128, 8, 512], BF16) for i in range(2)]
        xT2 = [S.sb("xTb%d" % i, [128, 8, 512], F32) for i in range(2)]
        sqt = S.sb("sqt6", [128, 8, 512], BF16)
        rstd = S.sb("rstd6", [128, 512], F32)
        tmpf = [S.sb("tmpf6_%d" % i, [128, 512], F32) for i in range(2)]
        h2T = [S.sb("h2T%d" % i, [128, 8, 512], BF16) for i in range(2)]
        fgroups = [g for g in range(NG) if not (last and groups[g][2] == 1)]

        def yt_view(g):
            t0, n, j = groups[g]
            res = [S.R("YTa", g, h) for h in range(6)] + [S.R("YTb", g)] + \
                  [S.R("YTc", qb, gg, r) for qb in range(t0 // 128, (t0 + n) // 128) for gg in range(2) for r in range(2)]
            return V(YT[:, :, t0:t0 + n].rearrange("k p t -> p k t"), res)

        def load6(g):
            n = groups[g][1]
            S.dma("sp", yT2[g % 2][:, :, :n], yt_view(g))
            S.dma("sp", xT2[g % 2][:, :, :n], xt_view(g))

        load6(fgroups[0])
        for fi, g in enumerate(fgroups):
            t0, n, j = groups[g]
            if fi + 1 < len(fgroups):
                load6(fgroups[fi + 1])
            yT, xT, hh2 = yT2[g % 2], xT2[g % 2], h2T[g % 2]
            for oc in range(8):
                ps = nps()
                for kc in range(8):
                    S.mm(ps[:, :n], Wo[:, kc, oc * 128:(oc + 1) * 128], yT[:, kc, :n], start=(kc == 0), stop=(kc == 7))
                S.stt("dve", xT[:, oc, :n], ps[:, :n], modT[:, 16 + oc, j:j + 1], xT[:, oc, :n], ALU.mult, ALU.add)
            S.dma("sp", xt_view(g), xT[:, :, :n])
            rms_rstd([xT[:, kc, :n] for kc in range(8)], n, D, sqt, rstd)
            for kc in range(8):
                tf = tmpf[kc % 2]
                S.stt("dve", tf[:, :n], xT[:, kc, :n], gsc[:, 1, kc, j:j + 1], rstd[:, :n], ALU.mult, ALU.mult)
                S.act(hh2[:, kc, :n], tf[:, :n], AF.Identity, bias=modT[:, 24 + kc, j:j + 1])
            S.dma("sp", dv(H2[:, :, t0:t0 + n].rearrange("k p t -> p k t"), "H2", g), hh2[:, :, :n])
        S.pop("S6a")

        if stop_after == "S6a":
            break
        S.push()
        alloc_wstg()
        Wu = S.sb("Wu", [128, 8, 2 * FFN], BF16)
        load_w_kc(Wu, w_up[l], 2 * FFN)
        h2l = [S.sb("h2l%d" % i, [128, 8, 512], BF16) for i in range(2)]
        gst = [S.sb("gst%d" % i, [128, 4, 512], BF16) for i in range(3)]

        def loadh(g):
            t0, n, j = groups[g]
            S.dma("sp", h2l[g % 2][:, :, :n], dv(H2[:, :, t0:t0 + n].rearrange("k p t -> p k t"), "H2", g))

        loadh(fgroups[0])
        sc_ = 0
        for fi, g in enumerate(fgroups):
            t0, n, j = groups[g]
            if fi + 1 < len(fgroups):
                loadh(fgroups[fi + 1])
            hh2 = h2l[g % 2]
            for u4 in range(2 * FC // 4):
                st = gst[sc_ % 3]
                sc_ += 1
                for u in range(4):
                    uc = u4 * 4 + u
                    ps = nps()
                    for kc in range(8):
                        S.mm(ps[:, :n], Wu[:, kc, uc * 128:(uc + 1) * 128], hh2[:, kc, :n], start=(kc == 0), stop=(kc == 7))
                    S.copy("act" if u % 2 == 0 else "dve", st[:, u, :n], ps[:, :n])
                S.dma("sp", dv(GV[u4 * 4:u4 * 4 + 4, :, t0:t0 + n].rearrange("c p t -> p c t"), "GV", g, u4), st[:, :, :n])
        S.pop("S6b")

        if stop_after == "S6b":
            break
        S.push()
        alloc_wstg()
        Wd = S.sb("Wd", [128, FC, D], BF16)
        for c in range(FC):
            load_w(Wd[:, c, :], w_dn[l, c * 128:(c + 1) * 128, :], D)
        fcw = S.sb("fcw", [128, FC, 3], F32)
        fcb = S.sb("fcb", [128, FC], F32)
        S.dma("sp", fcw, fcw_d[l])
        S.dma("sp", fcb, fcb_d[l])
        gat2 = [S.sb("gat%d" % i, [128, FC, 514], BF16) for i in range(2)]
        val2 = [S.sb("val%d" % i, [128, FC, 512], BF16) for i in range(2)]
        xT7 = S.sb("xT7", [128, 8, 512], F32)
        su = [S.sb("su%d" % i, [128, 512], BF16) for i in range(2)]
        gvres = [S.R("GV", g, u4) for g in range(NG) for u4 in range(2 * FC // 4)]
        dg = S.sb("dg", [128, FC, 3, 128], BF16)
        for c in range(FC):
            for jt in range(3):
                S.ts("dve" if (c + jt) % 2 == 0 else "pool", dg[:, c, jt, :], identb, fcw[:, c, jt:jt + 1], None, op0=ALU.mult)

        def load7(fi):
            g = fgroups[fi]
            t0, n, j = groups[g]
            s0, s1 = (0, CTX) if j == 1 else (CTX, T)
            gat, val = gat2[fi % 2], val2[fi % 2]
            lo = max(t0 - 1, s0)
            hi = min(t0 + n + 1, s1)
            if lo == t0:
                S.memset("pool", gat[:, :, 0:1], 0.0)
            if hi == t0 + n:
                S.memset("pool", gat[:, :, n + 1:n + 2], 0.0)
            S.dma("sp", gat[:, :, lo - (t0 - 1):hi - (t0 - 1)], V(GV[FC:2 * FC, :, lo:hi].rearrange("c p t -> p c t"), gvres))
            S.dma("sp", val[:, :, :n], V(GV[0:FC, :, t0:t0 + n].rearrange("c p t -> p c t"), gvres))

        load7(0)
        for fi, g in enumerate(fgroups):
            t0, n, j = groups[g]
            gat, val = gat2[fi % 2], val2[fi % 2]
            xT = xT7
            S.dma("sp", xT[:, :, :n], xt_view(g))
            if fi + 1 < len(fgroups):
                load7(fi + 1)
            for c in range(FC):
                s_ = su[c % 2]
                psc = nps()
                for jt in range(3):
                    S.mm(psc[:, :n], dg[:, c, jt, :], gat[:, c, jt:jt + n], start=(jt == 0), stop=(jt == 2))
                S.act(s_[:, :n], psc[:, :n], AF.Silu, bias=fcb[:, c:c + 1])
                S.tt("dve", val[:, c, :n], s_[:, :n], val[:, c, :n], ALU.mult)
            for oc in range(8):
                ps = nps()
                for c in range(FC):
                    S.mm(ps[:, :n], Wd[:, c, oc * 128:(oc + 1) * 128], val[:, c, :n], start=(c == 0), stop=(c == FC - 1))
                S.stt("dve", xT[:, oc, :n], ps[:, :n], modT[:, 40 + oc, j:j + 1], xT[:, oc, :n], ALU.mult, ALU.add)
            S.dma("sp", xt_view(g), xT[:, :, :n])
        S.pop("S7")

    if stop_after is not None:
        S.finish()
        return nc, S
    S.push()
    fng = S.sb("fng", [128, 8], F32)
    S.dma("sp", fng, fng_d)
    xT2 = [S.sb("xT8_%d" % i, [128, 8, 512], F32) for i in range(2)]
    sqt = S.sb("sqt8", [128, 8, 512], BF16)
    rstd = S.sb("rstd8", [128, 512], F32)
    xn = S.sb("xn", [128, 8, 512], F32)
    ot = [S.sb("ot%d" % i, [128, D], F32) for i in range(2)]
    oi = 0
    for g, (t0, n, j) in enumerate(groups):
        if j == 1:
            continue
        xT = xT2[g % 2]
        S.dma("sp", xT[:, :, :n], xt_view(g))
        rms_rstd([xT[:, kc, :n] for kc in range(8)], n, D, sqt, rstd)
        for kc in range(8):
            S.stt("dve", xn[:, kc, :n], xT[:, kc, :n], fng[:, kc:kc + 1], rstd[:, :n], ALU.mult, ALU.mult)
        for tb in range(n // 128):
            o = ot[oi % 2]
            oi += 1
            for half in range(2):
                ps = nps()
                for k4 in range(4):
                    kc = half * 4 + k4
                    S.tr(ps[:, k4 * 128:(k4 + 1) * 128], xn[:, kc, tb * 128:(tb + 1) * 128], ident)
                S.copy("dve" if half == 0 else "act", o[:, half * 512:(half + 1) * 512], ps)
            r0 = t0 - CTX + tb * 128
            S.dma("sp", dv(out_d[r0:r0 + 128, :], "OUT", r0), o)
    S.pop("S8")
    S.finish()
    return nc, S


_CACHE = {}


def kernel(**inputs):
    nlat = inputs["x"].shape[1]
    depth = inputs["w_mod"].shape[0]
    ncores = inputs["x"].shape[0]
    maps = prep_inputs(inputs, nlat, depth, ncores)
    nc, S = build_program(nlat, depth)
    res = run_bass_kernel_spmd(nc, maps, core_ids=list(range(ncores)))
    out = np.stack([np.asarray(res.results[b]["out"], np.float32) for b in range(ncores)], axis=0)
    return out
```
